# Optimizing a Trainium2 kernel written in Bass

```python
import jax, jax.numpy as jnp
from jax import lax
import numpy as np

D_MODEL = 2048
BATCH = 4
SEQ = 2048
DEPTH = 1

GDN_HEADS = 16
GDN_HEAD_K = 128
GDN_HEAD_V = 128
GDN_CONV = 4
GDN_CHUNK = 64
QK_W = GDN_HEADS * GDN_HEAD_K
V_W = GDN_HEADS * GDN_HEAD_V
SC_WIDTH = 2048
SC_CONV = 3
D_FF = -(-8 * D_MODEL // (3 * 256)) * 256
EPS = 1e-6

IN_SIZES = (QK_W, QK_W, V_W, V_W, GDN_HEADS, GDN_HEADS, SC_WIDTH, SC_WIDTH, SC_WIDTH, D_MODEL, D_MODEL)
IN_WIDTH = sum(IN_SIZES)

kernel_name = "hybrid_gdn_shortconv_gated_merge"


def rms_norm(x, g):
    xf = x.astype(jnp.float32)
    y = xf * lax.rsqrt(jnp.mean(xf * xf, axis=-1, keepdims=True) + EPS)
    return (y * g.astype(jnp.float32)).astype(x.dtype)


def l2_norm(x):
    return x * lax.rsqrt(jnp.sum(x * x, axis=-1, keepdims=True) + EPS)


def partition(t, sizes):
    out, start = [], 0
    for s in sizes:
        out.append(t[..., start:start + s])
        start += s
    return out


def causal_dwconv(x, w):
    K = w.shape[0]
    T = x.shape[1]
    xp = jnp.pad(x, ((0, 0), (K - 1, 0), (0, 0)))
    y = xp[:, 0:T] * w[0]
    for i in range(1, K):
        y = y + xp[:, i:i + T] * w[i]
    return y


def gated_delta_chunked(q, k, v, g, beta):
    Bn, T, H, Dk = q.shape
    Dv = v.shape[-1]
    C = GDN_CHUNK
    N = T // C

    def to_chunks(t):
        t = jnp.moveaxis(t, 2, 1)
        return t.reshape(t.shape[:2] + (N, C) + t.shape[3:])

    q, k, v, g, beta = map(to_chunks, (q * (Dk ** -0.5), k, v, g, beta))
    g = jnp.cumsum(g, axis=-1)
    causal = jnp.tril(jnp.ones((C, C), dtype=bool))
    strict = jnp.tril(jnp.ones((C, C), dtype=bool), -1)
    decay = jnp.exp(jnp.where(causal, g[..., :, None] - g[..., None, :], -jnp.inf))
    k_beta = k * beta[..., None]
    a_low = jnp.where(strict, jnp.einsum('bhnid,bhnjd->bhnij', k_beta, k) * decay, 0.0)
    t_mat = a_low + jnp.eye(C, dtype=a_low.dtype)
    rhs = jnp.concatenate([v * beta[..., None], k_beta * jnp.exp(g)[..., None]], axis=-1)
    sol = lax.linalg.triangular_solve(t_mat, rhs, left_side=True, lower=True, unit_diagonal=True)
    u, w = sol[..., :Dv], sol[..., Dv:]
    attn_intra = jnp.einsum('bhnid,bhnjd->bhnij', q, k) * decay
    q_dec = q * jnp.exp(g)[..., None]
    g_last = g[..., -1]
    k_dec = k * jnp.exp(g_last[..., None] - g)[..., None]

    def step(S, xs):
        q_c, k_c, u_c, w_c, attn_c, gl_c = xs
        v_new = u_c - jnp.einsum('bhcd,bhdv->bhcv', w_c, S)
        o = jnp.einsum('bhcd,bhdv->bhcv', q_c, S) + jnp.einsum('bhij,bhjv->bhiv', attn_c, v_new)
        S = S * jnp.exp(gl_c)[..., None, None] + jnp.einsum('bhcd,bhcv->bhdv', k_c, v_new)
        return S, o

    xs = tuple(jnp.moveaxis(t, 2, 0) for t in (q_dec, k_dec, u, w, attn_intra, g_last))
    S0 = jnp.zeros((Bn, H, Dk, Dv), dtype=jnp.float32)
    _, o = lax.scan(step, S0, xs)
    o = jnp.moveaxis(o, 0, 2).reshape(Bn, H, T, Dv)
    return jnp.moveaxis(o, 1, 2)


def gdn_branch(q, k, v, z, b, a, conv_w, A_log, dt_bias, norm_g):
    Bn, T, _ = q.shape
    dtype = q.dtype
    qkv = jax.nn.silu(causal_dwconv(jnp.concatenate([q, k, v], axis=-1), conv_w))
    q, k, v = partition(qkv.astype(jnp.float32), (QK_W, QK_W, V_W))
    q = l2_norm(q.reshape(Bn, T, GDN_HEADS, GDN_HEAD_K))
    k = l2_norm(k.reshape(Bn, T, GDN_HEADS, GDN_HEAD_K))
    v = v.reshape(Bn, T, GDN_HEADS, GDN_HEAD_V)
    beta = jax.nn.sigmoid(b.astype(jnp.float32))
    g = -jnp.exp(A_log.astype(jnp.float32)) * jax.nn.softplus(a.astype(jnp.float32) + dt_bias.astype(jnp.float32))
    o = gated_delta_chunked(q, k, v, g, beta)
    o = rms_norm(o, norm_g) * jax.nn.silu(z.reshape(Bn, T, GDN_HEADS, GDN_HEAD_V).astype(jnp.float32))
    return o.reshape(Bn, T, V_W).astype(dtype)


def setup_inputs(seed: int = 0) -> dict:
    key = jax.random.key(seed)
    ks = jax.random.split(key, 17)
    L = DEPTH
    nrm = lambda k, shape, fan_in: jax.random.normal(k, shape, jnp.float32) * (fan_in ** -0.5)
    gain = lambda k, shape: 1.0 + 0.02 * jax.random.normal(k, shape, jnp.float32)
    return {
        "x": jax.random.normal(ks[0], (BATCH, SEQ, D_MODEL), jnp.float32),
        "ln_mix_g": gain(ks[1], (L, D_MODEL)),
        "w_in": nrm(ks[2], (L, D_MODEL, IN_WIDTH), D_MODEL),
        "conv_qkv_w": nrm(ks[3], (L, GDN_CONV, 2 * QK_W + V_W), GDN_CONV),
        "A_log": jnp.log(jax.random.uniform(ks[4], (L, GDN_HEADS), jnp.float32, 1.0, 16.0)),
        "dt_bias": 0.1 * jax.random.normal(ks[5], (L, GDN_HEADS), jnp.float32),
        "gdn_norm_g": gain(ks[6], (L, GDN_HEAD_V)),
        "w_proj_a": nrm(ks[7], (L, V_W, D_MODEL), V_W),
        "conv_sc_w": nrm(ks[8], (L, SC_CONV, SC_WIDTH), SC_CONV),
        "w_proj_b": nrm(ks[9], (L, SC_WIDTH, D_MODEL), SC_WIDTH),
        "w_out": nrm(ks[10], (L, D_MODEL, D_MODEL), D_MODEL),
        "ln_ffn_g": gain(ks[11], (L, D_MODEL)),
        "w_gate": nrm(ks[12], (L, D_MODEL, D_FF), D_MODEL),
        "w_up": nrm(ks[13], (L, D_MODEL, D_FF), D_MODEL),
        "w_down": nrm(ks[14], (L, D_FF, D_MODEL), D_FF),
        "ln_final_g": gain(ks[15], (D_MODEL,)),
    }


def reference(x, ln_mix_g, w_in, conv_qkv_w, A_log, dt_bias, gdn_norm_g, w_proj_a, conv_sc_w,
              w_proj_b, w_out, ln_ffn_g, w_gate, w_up, w_down, ln_final_g):
    h = x
    for l in range(DEPTH):
        xn = rms_norm(h, ln_mix_g[l])
        proj = jnp.einsum('btd,de->bte', xn, w_in[l])
        q, k, v, z, b, a, sc_b, sc_c, sc_h, gate_a, gate_b = partition(proj, IN_SIZES)
        o_a = gdn_branch(q, k, v, z, b, a, conv_qkv_w[l], A_log[l], dt_bias[l], gdn_norm_g[l])
        o_b = sc_b * causal_dwconv(sc_c * sc_h, conv_sc_w[l])
        merged = (jax.nn.sigmoid(gate_a) * jnp.einsum('btv,vd->btd', o_a, w_proj_a[l])
                  + jax.nn.sigmoid(gate_b) * jnp.einsum('btc,cd->btd', o_b, w_proj_b[l]))
        h = h + jnp.einsum('btd,de->bte', merged, w_out[l])
        hn = rms_norm(h, ln_ffn_g[l])
        ff = jax.nn.silu(jnp.einsum('btd,df->btf', hn, w_gate[l])) * jnp.einsum('btd,df->btf', hn, w_up[l])
        h = h + jnp.einsum('btf,fd->btd', ff, w_down[l])
    return rms_norm(h, ln_final_g)
```

```python
import numpy as np
from contextlib import ExitStack
import concourse.bass as bass
import concourse.mybir as mybir
from concourse.bass_utils import run_bass_kernel_spmd

F32 = mybir.dt.float32
BF16 = mybir.dt.bfloat16
AF = mybir.ActivationFunctionType
ALU = mybir.AluOpType

D = 2048
KC = 16
NH = 16
TOK = 1024
NG = 2048
DFF = 5632
NFC = 44
INW = 18464
C_Q, C_K, C_V, C_Z, C_B, C_A = 0, 2048, 4096, 6144, 8192, 8208
C_SB, C_SC, C_SH, C_GA, C_GB = 8224, 10272, 12320, 14368, 16416
EPS = 1e-6


class Buf:
    def __init__(self, ap, name=""):
        self.ap = ap
        self.name = name
        self.w = {}
        self.r = {}

    def __getitem__(self, k):
        return self.ap[k]


class Arena:
    def __init__(self, cap):
        self.cap = cap
        self.free_list = [(0, cap)]
        self.live = {}
        self.dead = []

    def alloc(self, n):
        n = (n + 63) // 64 * 64
        for i, (o, s) in enumerate(self.free_list):
            if s >= n:
                if s == n:
                    self.free_list.pop(i)
                else:
                    self.free_list[i] = (o + n, s - n)
                return o, n
        raise MemoryError(f"arena: cannot alloc {n}, free={self.free_list}")

    def release(self, off, n):
        self.free_list.append((off, n))
        self.free_list.sort()
        out = []
        for o, s in self.free_list:
            if out and out[-1][0] + out[-1][1] == o:
                out[-1] = (out[-1][0], out[-1][1] + s)
            else:
                out.append((o, s))
        self.free_list = out


class KB:
    def __init__(self, debug=()):
        self.debug = set(debug)
        self.dbg_out = {}
        self.nc = nc = bass.Bass("TRN2", target_bir_lowering=False)
        self.eng = {"pe": nc.tensor, "act": nc.scalar, "dve": nc.vector, "pool": nc.gpsimd, "sp": nc.sync}
        self.sem = {}
        self.cnt = {}
        for e in ["pe", "act", "dve", "pool"]:
            self.sem[e] = nc.alloc_semaphore("s_" + e)
            self.cnt[e] = 0
        self.dq = {}
        for q in ["sp", "pool"]:
            sems = [nc.alloc_semaphore(f"s_dma_{q}{i}") for i in range(6)]
            self.dq[q] = {"sems": sems, "cnt": [0] * 6, "next": 0}
        self.seen = {}
        self.pe_pending = []
        cap = (nc.sbuf_bytes_remaining - 256) // 64 * 64
        self.arena_t = nc.alloc_sbuf_tensor("arena", [128, cap // 2], BF16)
        self.arena = Arena(cap)
        self.nwaits = 0

    def sb(self, name, shape, dtype):
        esz = 4 if dtype == F32 else 2
        n = int(np.prod(shape)) * esz
        off, n2 = self.arena.alloc(n)
        v = self.arena_t[:, off // 2:(off + n) // 2]
        if dtype == F32:
            v = v.bitcast(F32)
        if len(shape) == 2:
            v = v.rearrange("p (a b) -> p a b", a=shape[0])
        elif len(shape) == 3:
            v = v.rearrange("p (a b c) -> p a b c", a=shape[0], b=shape[1])
        elif len(shape) == 4:
            v = v.rearrange("p (a b c d) -> p a b c d", a=shape[0], b=shape[1], c=shape[2])
        b = Buf(v, name)
        b.off, b.n = off, n2
        keep = []
        for (o, s, ob) in self.arena.dead:
            if o < off + n2 and off < o + s:
                for k, val in ob.w.items():
                    b.r[k] = max(b.r.get(k, 0), val)
                for k, val in ob.r.items():
                    b.r[k] = max(b.r.get(k, 0), val)
                keep.append((o, s, ob))
            else:
                keep.append((o, s, ob))
        self.arena.dead = keep
        return b

    def free(self, *bufs):
        for b in bufs:
            self.arena.release(b.off, b.n)
            self.arena.dead.append((b.off, b.n, b))

    def _semof(self, key):
        if isinstance(key, str):
            return self.sem[key]
        q, i = key
        return self.dq[q]["sems"][i]

    def _wait(self, e, key, val):
        if e == "pe" and key == "pe":
            return
        seen = self.seen.setdefault(e, {})
        if seen.get(key, 0) >= val:
            return
        seen[key] = val
        self.eng[e].wait_ge(self._semof(key), val)
        self.nwaits += 1

    def _deps(self, e, R, W):
        for b in R:
            for k, v in b.w.items():
                self._wait(e, k, v)
            if getattr(b, "psum", False):
                for k, v in b.r.items():
                    if k != e:
                        self._wait(e, k, v)
        for b in W:
            for k, v in b.w.items():
                self._wait(e, k, v)
            for k, v in b.r.items():
                self._wait(e, k, v)

    def _mark(self, key, val, R, W):
        for b in R:
            b.r[key] = max(b.r.get(key, 0), val)
        for b in W:
            b.w[key] = max(b.w.get(key, 0), val)
            b.r = {}

    def op(self, e, fn, R=(), W=(), sig=True):
        self._deps(e, R, W)
        ins = fn()
        if e == "pe" and not sig:
            self.pe_pending.append((R, W))
            return
        self.cnt[e] += 1
        ins.then_inc(self.sem[e], 1)
        if e == "pe" and self.pe_pending:
            for (r2, w2) in self.pe_pending:
                self._mark(e, self.cnt[e], r2, w2)
            self.pe_pending = []
        self._mark(e, self.cnt[e], R, W)

    def dma(self, q, out, in_, R=(), W=()):
        dq = self.dq[q]
        i = dq["next"]
        dq["next"] = (i + 1) % len(dq["sems"])
        if dq["cnt"][i] > 0:
            self._wait(q, (q, i), dq["cnt"][i])
        self._deps(q, R, W)
        self.eng[q].dma_start(out=out, in_=in_).then_inc(dq["sems"][i], 16)
        dq["cnt"][i] += 16
        self._mark((q, i), dq["cnt"][i], R, W)

    def drain(self):
        for q in self.dq:
            for i, c in enumerate(self.dq[q]["cnt"]):
                if c:
                    self._wait("sp", (q, i), c)
        for e in ["pe", "act", "dve", "pool"]:
            if self.cnt[e]:
                self._wait("sp", e, self.cnt[e])

    def mm(self, out, lhsT, rhs, start, stop, R, W, sig=True):
        nc = self.nc
        self.op("pe", lambda: nc.tensor.matmul(out, lhsT=lhsT, rhs=rhs, start=start, stop=stop), R, W, sig)

    def tr(self, out, in_, ident, R, W, sig=True):
        nc = self.nc
        self.op("pe", lambda: nc.tensor.transpose(out=out, in_=in_, identity=ident), R, W, sig)

    def act(self, out, in_, func, R, W, bias=None, scale=None, accum_out=None):
        nc = self.nc
        kw = {}
        if bias is not None:
            kw["bias"] = bias
        if scale is not None:
            kw["scale"] = scale
        if accum_out is not None:
            kw["accum_out"] = accum_out
        self.op("act", lambda: nc.scalar.activation(out=out, in_=in_, func=func, **kw), R, W)

    def tt(self, e, out, in0, in1, op, R, W):
        E = self.eng[e]
        self.op(e, lambda: E.tensor_tensor(out=out, in0=in0, in1=in1, op=op), R, W)

    def ts(self, e, out, in0, s1, s2, op0, op1, R, W):
        E = self.eng[e]
        if op1 is None:
            self.op(e, lambda: E.tensor_scalar(out=out, in0=in0, scalar1=s1, scalar2=None, op0=op0), R, W)
        else:
            self.op(e, lambda: E.tensor_scalar(out=out, in0=in0, scalar1=s1, scalar2=s2, op0=op0, op1=op1), R, W)

    def stt(self, e, out, in0, scalar, in1, op0, op1, R, W):
        E = self.eng[e]
        self.op(e, lambda: E.scalar_tensor_tensor(out=out, in0=in0, scalar=scalar, in1=in1, op0=op0, op1=op1), R, W)

    def copy(self, e, out, in_, R, W):
        if e == "act":
            self.act(out, in_, AF.Copy, R, W)
        else:
            E = self.eng[e]
            self.op(e, lambda: E.tensor_copy(out=out, in_=in_), R, W)

    def memset(self, e, ap, val, W):
        E = self.eng[e]
        self.op(e, lambda: E.memset(ap, val), (), W)

    def recip(self, out, in_, R, W):
        nc = self.nc
        self.op("dve", lambda: nc.vector.reciprocal(out=out, in_=in_), R, W)

    def dump(self, name, buf, shape, dtype=F32):
        if name not in self.debug:
            return
        t = self.nc.dram_tensor("dbg_" + name, [128] + list(shape), dtype, kind="ExternalOutput").ap()
        self.dbg_out[name] = "dbg_" + name
        self.dma("sp", t, buf.ap, R=[buf])


def build(debug=(), ntasks=None, stop_after_gdn=False, tasklist=None):
    kb = KB(debug)
    nc = kb.nc
    DI = lambda n, s: nc.dram_tensor(n, s, F32, kind="ExternalInput").ap()
    xg = DI("xg", [NG, D])
    w_in = DI("w_in", [D, INW])
    w_pa = DI("w_proj_a", [D, D])
    w_pb = DI("w_proj_b", [D, D])
    w_out = DI("w_out", [D, D])
    w_gate = DI("w_gate", [D, DFF])
    w_up = DI("w_up", [D, DFF])
    w_down = DI("w_down", [DFF, D])
    gmix_row_d = DI("gmix_row", [128, D])
    gffn_d = DI("gffn_p", [128, KC])
    gfin_d = DI("gfin_p", [128, KC])
    cwqkv_d = DI("cw_qkv", [128, 48, 4])
    cwsc_d = DI("cw_sc", [128, KC, 3])
    alog_d = DI("alog_row", [128, NH])
    dtb_d = DI("dtb_row", [128, NH])
    gng_d = DI("gng_p", [128, 1])
    y = nc.dram_tensor("y", [TOK, D], F32, kind="ExternalOutput").ap()

    PS = []
    for i in range(4):
        t = nc.alloc_psum_tensor(f"ps{i}", [128, 1024], F32)
        PS.append(Buf(t[:, :], f"ps{i}"))
        PS[-1].psum = True

    def psf(i, shape=None):
        return PS[i].ap

    def psb(i):
        return PS[i].ap.bitcast(BF16)

    ident_b = kb.sb("ident_b", [128], BF16)
    ident_f = kb.sb("ident_f", [128], F32)
    ones_b = kb.sb("ones_b", [128], BF16)
    ones_f = kb.sb("ones_f", [128], F32)
    tri_f = kb.sb("tri_f", [128], F32)
    sl_f = kb.sb("sl_f", [128], F32)
    li_f = kb.sb("li_f", [128], F32)
    eps_t = kb.sb("eps_t", [1], F32)
    gffn = kb.sb("gffn", [KC], F32)
    gfin = kb.sb("gfin", [KC], F32)
    cwqkv = kb.sb("cwqkv", [48, 4], F32)
    cwsc = kb.sb("cwsc", [KC, 3], F32)
    negA = kb.sb("negA", [NH], F32)
    dtb = kb.sb("dtb", [NH], F32)
    gng = kb.sb("gng", [1], F32)

    def sel(buf, pattern, cm, cmp):
        kb.memset("pool", buf.ap, 1.0, [buf])
        kb.op("pool", lambda: nc.gpsimd.affine_select(out=buf.ap, in_=buf.ap, pattern=pattern, compare_op=cmp,
                                                      fill=0.0, base=0, channel_multiplier=cm), [buf], [buf])

    sel(ident_f, [[-1, 128]], 1, ALU.is_equal)
    sel(tri_f, [[1, 128]], -1, ALU.is_ge)
    sel(sl_f, [[-1, 128]], 1, ALU.is_gt)
    sel(li_f, [[-1, 128]], 1, ALU.is_ge)
    kb.memset("pool", ones_f.ap, 1.0, [ones_f])
    kb.memset("pool", ones_b.ap, 1.0, [ones_b])
    kb.memset("pool", eps_t.ap, EPS, [eps_t])
    kb.copy("dve", ident_b.ap, ident_f.ap, [ident_f], [ident_b])
    kb.dma("sp", gffn.ap, gffn_d, W=[gffn])
    kb.dma("sp", gfin.ap, gfin_d, W=[gfin])
    kb.dma("sp", cwqkv.ap, cwqkv_d, W=[cwqkv])
    kb.dma("sp", cwsc.ap, cwsc_d, W=[cwsc])
    kb.dma("sp", negA.ap, alog_d, W=[negA])
    kb.dma("sp", dtb.ap, dtb_d, W=[dtb])
    kb.dma("sp", gng.ap, gng_d, W=[gng])
    kb.act(negA.ap, negA.ap, AF.Exp, [negA], [negA])
    kb.ts("dve", negA.ap, negA.ap, -1.0, None, ALU.mult, None, [negA], [negA])

    WR = [kb.sb(f"wr{i}", [8192], BF16) for i in range(2)]
    wr_state = {"i": 0}

    def wslot():
        b = WR[wr_state["i"] % len(WR)]
        wr_state["i"] += 1
        return b

    def wdma(dst, src, slot):
        kb.dma("pool", dst, src, W=[slot])

    w_in_v = w_in.rearrange("(k p) n -> p k n", p=128)

    XN = kb.sb("XN", [KC, TOK], BF16)
    XNP = kb.sb("XNP", [KC, TOK], BF16)
    gmix = kb.sb("gmix", [D], F32)
    kb.dma("sp", gmix.ap, gmix_row_d, W=[gmix])
    xt = [kb.sb(f"xt{i}", [D], F32) for i in range(2)]
    xs = [kb.sb(f"xs{i}", [D], BF16) for i in range(2)]
    junk = kb.sb("junk", [D], BF16)
    ssq = kb.sb("ssq", [16, 2], F32)

    for t in range(16):
        X_t, S_t = xt[t % 2], xs[t % 2]
        dst = XNP if t < 8 else XN
        tl = t % 8
        kb.dma("sp", X_t.ap, xg[t * 128:(t + 1) * 128, :], W=[X_t])
        kb.act(junk.ap, X_t.ap, AF.Square, [X_t], [junk, ssq], accum_out=ssq[:, t, 0:1])
        kb.act(ssq[:, t, 1:2], ssq[:, t, 0:1], AF.Sqrt, [ssq, eps_t], [ssq], bias=eps_t[:, 0:1], scale=1.0 / D)
        kb.recip(ssq[:, t, 1:2], ssq[:, t, 1:2], [ssq], [ssq])
        kb.stt("dve", S_t.ap, X_t.ap, ssq[:, t, 1:2], gmix.ap, ALU.mult, ALU.mult, [X_t, ssq, gmix], [S_t])
        for g in range(2):
            pb = PS[(2 * t + g) % 4]
            pv = psb((2 * t + g) % 4)[:, 0:1024].rearrange("p (a b) -> p a b", a=8)
            for j in range(8):
                kc = g * 8 + j
                kb.tr(pv[:, j, :], S_t[:, kc * 128:(kc + 1) * 128], ident_b.ap, [S_t, ident_b], [pb], sig=(j == 7))
            kb.copy("act" if g == 0 else "dve", dst[:, g * 8:(g + 1) * 8, tl * 128:(tl + 1) * 128], pv, [pb], [dst])
    kb.free(gmix, junk, ssq, *xt, *xs)
    kb.dump("XN", XN, [KC, TOK], BF16)
    kb.dump("XNP", XNP, [KC, TOK], BF16)

    XH = kb.sb("XH", [KC, 2], BF16)
    kb.copy("dve", XH.ap, XNP[:, :, 1022:1024], [XNP], [XH])

    def xn_tok(kc, t0, n):
        if t0 + n <= 1024:
            return XNP[:, kc, t0:t0 + n]
        assert t0 >= 1024
        return XN[:, kc, t0 - 1024:t0 - 1024 + n]

    def xn_buf(t0):
        return XNP if t0 < 1024 else XN

    wba_slot = wslot()
    wba = wba_slot.ap[:, 0:KC * 32].rearrange("p (k n) -> p k n", k=KC)
    wdma(wba, w_in_v[:, :, C_B:C_B + 32], wba_slot)
    ba = kb.sb("ba", [16, 32], F32)
    beta = kb.sb("beta", [16, NH], F32)
    nb = kb.sb("nb", [16, NH], F32)
    gtok = kb.sb("gtok", [16, NH], F32)
    gcum = kb.sb("gcum", [16, NH], F32)
    nbg = kb.sb("nbg", [16, NH], F32)
    dkd = kb.sb("dkd", [16, NH], F32)
    egl = kb.sb("egl", [16, NH], F32)
    tmp1 = kb.sb("tmp1", [16, NH], F32)
    tmp2 = kb.sb("tmp2", [16, NH], F32)
    pba = psf(0)[:, 0:512].rearrange("p (c n) -> p c n", c=16)
    for c in range(16):
        for kc in range(KC):
            kb.mm(pba[:, c, :], xn_tok(kc, c * 128, 128), wba[:, kc, :], kc == 0, kc == KC - 1,
                  [xn_buf(c * 128), wba_slot], [PS[0]], sig=(kc == KC - 1))
    kb.copy("act", ba.ap, pba, [PS[0]], [ba])
    kb.act(beta.ap, ba[:, :, 0:16], AF.Sigmoid, [ba], [beta])
    kb.ts("dve", nb.ap, beta.ap, -1.0, None, ALU.mult, None, [beta], [nb])
    dtb_b = dtb.ap.unsqueeze(1).to_broadcast([128, 16, NH])
    negA_b = negA.ap.unsqueeze(1).to_broadcast([128, 16, NH])
    kb.tt("dve", tmp1.ap, ba[:, :, 16:32], dtb_b, ALU.add, [ba, dtb], [tmp1])
    kb.act(tmp2.ap, tmp1.ap, AF.Abs, [tmp1], [tmp2])
    kb.act(tmp2.ap, tmp2.ap, AF.Exp, [tmp2], [tmp2], scale=-1.0)
    kb.act(tmp2.ap, tmp2.ap, AF.Ln, [tmp2], [tmp2], bias=1.0)
    kb.ts("dve", tmp1.ap, tmp1.ap, 0.0, None, ALU.max, None, [tmp1], [tmp1])
    kb.tt("dve", tmp1.ap, tmp1.ap, tmp2.ap, ALU.add, [tmp1, tmp2], [tmp1])
    kb.tt("dve", gtok.ap, tmp1.ap, negA_b, ALU.mult, [tmp1, negA], [gtok])
    gflat = gtok.ap.rearrange("p c h -> p (c h)")
    pg = psf(1)
    kb.mm(pg[:, 0:256], tri_f.ap, gflat, True, True, [tri_f, gtok], [PS[1]])
    kb.mm(pg[:, 512:768], ones_f.ap, gflat, True, True, [ones_f, gtok], [PS[1]])
    f2 = lambda b: b.ap.rearrange("p c h -> p (c h)")
    kb.copy("act", f2(gcum), pg[:, 0:256], [PS[1]], [gcum])
    kb.act(f2(egl), pg[:, 512:768], AF.Exp, [PS[1]], [egl])
    kb.tt("dve", f2(tmp1), pg[:, 512:768], f2(gcum), ALU.subtract, [PS[1], gcum], [tmp1])
    kb.act(f2(dkd), f2(tmp1), AF.Exp, [tmp1], [dkd])
    kb.act(f2(tmp2), f2(gcum), AF.Exp, [gcum], [tmp2])
    kb.tt("dve", f2(nbg), f2(nb), f2(tmp2), ALU.mult, [nb, tmp2], [nbg])
    kb.free(ba, tmp1, tmp2, gcum)
    kb.dump("beta", beta, [16, NH])
    kb.dump("gtok", gtok, [16, NH])
    kb.dump("dkd", dkd, [16, NH])
    kb.dump("nbg", nbg, [16, NH])

    def inherit(dsts, srcs):
        for dd in dsts:
            for ss in srcs:
                for k_, v_ in list(ss.w.items()) + list(ss.r.items()):
                    dd.r[k_] = max(dd.r.get(k_, 0), v_)

    PB = [Buf(PS[i // 2].ap[:, (i % 2) * 512:(i % 2 + 1) * 512], f"pb{i}") for i in range(8)]
    inherit(PB, PS)
    for b_ in PB:
        b_.psum = True

    def pbb(i):
        return PB[i].ap.bitcast(BF16)

    SST = kb.sb("SST", [NH, 128], F32)
    HALO = kb.sb("HALO", [NH, 3, 3], F32)
    kb.memset("dve", SST.ap, 0.0, [SST])
    kb.memset("dve", HALO.ap, 0.0, [HALO])
    RAWS = [kb.sb(f"raw{i}", [3 + 1024], F32) for i in range(2)]
    CVS = [kb.sb(f"cv{i}", [1024], F32) for i in range(2)]
    SQS = [kb.sb(f"sq{i}", [1024], BF16) for i in range(2)]
    RN = kb.sb("rn", [1024], F32)
    SETA = [dict(QS=kb.sb(f"qs{i}", [1024], BF16), KT=kb.sb(f"kt{i}", [1024], BF16), VT=kb.sb(f"vt{i}", [1024], BF16),
                 ZS=kb.sb(f"zs{i}", [1024], F32)) for i in range(2)]
    BV = kb.sb("bv", [8, 128], BF16)
    KD = kb.sb("kd", [8, 128], BF16)
    CSET = [dict(QD=kb.sb(f"qd{i}", [1024], BF16), PM=kb.sb(f"pm{i}", [8, 128], BF16),
                 AT=kb.sb(f"at{i}", [8, 128], BF16)) for i in range(2)]
    SQ2 = kb.sb("sq2", [1024], BF16)
    RN2 = RN
    BSET = [dict(GBC=kb.sb(f"gbc{i}", [4, 128], F32), ET=kb.sb(f"et{i}", [4, 128], F32),
                 RG=kb.sb(f"rg{i}", [4, 128], F32), ATN=kb.sb(f"atn{i}", [4, 128], BF16),
                 NM=[kb.sb(f"nm{i}_{j}", [4, 2, 128], BF16) for j in range(2)]) for i in range(2)]
    SB_ = kb.sb("sbf", [128], BF16)
    ZB = kb.sb("zb", [128], BF16)
    VN = kb.sb("vn", [128], BF16)
    OT = kb.sb("ot", [1024], F32)
    gdn_state = {"OA": None}

    def prefetch_head(half, h):
        slot = wslot()
        WH = slot.ap.rearrange("p (k m n) -> p k m n", k=KC, m=4)
        nm_ = 4 if half == 1 else 3
        for m, cbase in enumerate([C_Q, C_K, C_V, C_Z][:nm_]):
            wdma(WH[:, :, m, :], w_in_v[:, :, cbase + h * 128:cbase + (h + 1) * 128], slot)
        return slot, WH

    a_k = {"k": 0}

    def stageA(half, h, slot, WH, SA_, steps):
        QS, KT, VT, ZS = SA_["QS"], SA_["KT"], SA_["VT"], SA_["ZS"]
        T0 = half * 1024
        XB_ = xn_buf(T0)
        nm_ = 4 if half == 1 else 3
        for kind, m in steps:
            if m >= nm_:
                continue
            RAW = RAWS[m % 2]
            CV = CVS[m % 2]
            SQ = SQS[m % 2]
            if kind == "proj":
                if m < 3:
                    kb.copy("dve", RAW[:, 0:3], HALO[:, h, m, :], [HALO], [RAW])
                for tt in range(2):
                    pi = a_k["k"] % 2
                    a_k["k"] += 1
                    for kc in range(KC):
                        kb.mm(PB[pi].ap, WH[:, kc, m, :], xn_tok(kc, T0 + tt * 512, 512),
                              kc == 0, kc == KC - 1, [slot, XB_], [PB[pi]], sig=(kc == KC - 1))
                    if m == 3:
                        kb.act(ZS[:, tt * 512:(tt + 1) * 512], PB[pi].ap, AF.Silu, [PB[pi]], [ZS])
                    else:
                        kb.copy("act", RAW[:, 3 + tt * 512:3 + (tt + 1) * 512], PB[pi].ap, [PB[pi]], [RAW])
                    yield
                if m < 3 and half == 0:
                    kb.copy("dve", HALO[:, h, m, :], RAW[:, 1024:1027], [RAW], [HALO])
                continue
            if kind == "norm":
                for tt in range(2):
                    pi = a_k["k"] % 2
                    a_k["k"] += 1
                    tsl = slice(tt * 512, (tt + 1) * 512)
                    kb.mm(PB[pi].ap, ones_b.ap, SQ[:, tsl], True, True, [ones_b, SQ], [PB[pi]])
                    kb.act(RN[:, tsl], PB[pi].ap, AF.Ln, [PB[pi], eps_t], [RN], bias=eps_t[:, 0:1])
                yield
                kb.act(RN.ap, RN.ap, AF.Exp, [RN], [RN], scale=-0.5)
                if m == 0:
                    kb.stt("dve", QS.ap, CV.ap, float(128 ** -0.5), RN.ap, ALU.mult, ALU.mult, [CV, RN], [QS])
                else:
                    kb.tt("dve", KT.ap, CV.ap, RN.ap, ALU.mult, [CV, RN], [KT])
                yield
                continue
            assert kind == "post" and m < 3
            wcs = [cwqkv[:, m * 16 + h, i:i + 1] for i in range(4)]
            kb.ts("dve", CV.ap, RAW[:, 0:1024], wcs[0], None, ALU.mult, None, [RAW, cwqkv], [CV])
            for i in range(1, 4):
                kb.stt("dve", CV.ap, RAW[:, i:i + 1024], wcs[i], CV.ap, ALU.mult, ALU.add, [RAW, cwqkv, CV], [CV])
                yield
            if m == 2:
                kb.act(VT.ap, CV.ap, AF.Silu, [CV], [VT])
                yield
                continue
            kb.act(CV.ap, CV.ap, AF.Silu, [CV], [CV])
            kb.act(SQ.ap, CV.ap, AF.Square, [CV], [SQ])
            yield

    def stageB_batch(half, h, cb, SA_, CS_):
        QS, KT = SA_["QS"], SA_["KT"]
        QD, PM, AT = CS_["QD"], CS_["PM"], CS_["AT"]
        bs = BSET[cb]
        GBC, ET, RG, ATN, NM = bs["GBC"], bs["ET"], bs["RG"], bs["ATN"], bs["NM"]
        X, Y, Z = (PB[2], PB[3], PB[4]) if cb == 0 else (PB[5], PB[6], PB[4])
        C0 = half * 8
        cs = cb * 4
        CG = C0 + cs
        tsl = slice(cs * 128, (cs + 4) * 128)
        v4 = lambda ap: ap.rearrange("p (c n) -> p c n", c=4)
        gsc = gtok[:, CG:CG + 4, h].unsqueeze(2).to_broadcast([128, 4, 128])
        kb.tt("dve", RG.ap, tri_f.ap.unsqueeze(1).to_broadcast([128, 4, 128]), gsc, ALU.mult, [tri_f, gtok], [RG])
        yield
        kb.mm(X.ap, ones_f.ap, RG.ap.rearrange("p c n -> p (c n)"), True, True, [ones_f, RG], [X])
        for c in range(4):
            kb.mm(v4(Y.ap)[:, c, :], RG[:, c, :], sl_f.ap, True, True, [RG, sl_f], [Y], sig=(c == 3))
        yield
        kb.act(GBC.ap.rearrange("p c n -> p (c n)"), X.ap, AF.Exp, [X], [GBC])
        kb.act(ET.ap, v4(Y.ap), AF.Exp, [Y], [ET])
        yield
        for c in range(4):
            ks = slice((cs + c) * 128, (cs + c + 1) * 128)
            kb.mm(v4(X.ap)[:, c, :], KT[:, ks], KT[:, ks], True, True, [KT], [X], sig=(c == 3))
        for c in range(4):
            ks = slice((cs + c) * 128, (cs + c + 1) * 128)
            kb.mm(v4(Y.ap)[:, c, :], QS[:, ks], KT[:, ks], True, True, [KT, QS], [Y], sig=(c == 3))
        yield
        kb.tt("dve", QD[:, tsl], QS[:, tsl], GBC.ap.rearrange("p c n -> p (c n)"), ALU.mult, [QS, GBC], [QD])
        ETM = GBC
        kb.tt("dve", ETM.ap, ET.ap, li_f.ap.unsqueeze(1).to_broadcast([128, 4, 128]), ALU.mult, [ET, li_f, GBC], [ETM])
        yield
        kb.tt("dve", ET.ap, ET.ap, sl_f.ap.unsqueeze(1).to_broadcast([128, 4, 128]), ALU.mult, [ET, sl_f], [ET])
        kb.tt("dve", ET.ap, ET.ap, nb[:, CG:CG + 4, h].unsqueeze(2).to_broadcast([128, 4, 128]), ALU.mult,
              [ET, nb], [ET])
        yield
        kb.tt("dve", NM[0][:, :, 0, :], v4(X.ap), ET.ap, ALU.mult, [X, ET], [NM[0]])
        kb.tt("dve", ATN.ap, v4(Y.ap), ETM.ap, ALU.mult, [Y, ETM], [ATN])
        yield
        pt = Z.ap.bitcast(BF16).rearrange("p (a c n) -> p a c n", a=2, c=4)
        for c in range(4):
            kb.tr(pt[:, 0, c, :], NM[0][:, c, 0, :], ident_b.ap, [NM[0], ident_b], [Z], sig=False)
        for c in range(4):
            kb.tr(pt[:, 1, c, :], ATN[:, c, :], ident_b.ap, [ATN, ident_b], [Z], sig=(c == 3))
        kb.copy("act", NM[0][:, :, 1, :], pt[:, 0], [Z], [NM[0]])
        kb.copy("act", AT[:, cs:cs + 4, :], pt[:, 1], [Z], [AT])
        kb.tt("dve", PM[:, cs:cs + 4, :], NM[0][:, :, 1, :],
              ident_b.ap.unsqueeze(1).to_broadcast([128, 4, 128]), ALU.add, [NM[0], ident_b], [PM])
        yield
        for step in range(1, 7):
            cur, nxt = NM[(step - 1) % 2], NM[step % 2]
            last = (step == 6)
            for c in range(4):
                kb.mm(v4(X.ap)[:, c, :], cur[:, c, 1, :], cur[:, c, 0, :], True, True, [cur], [X], sig=(c == 3))
            if not last:
                for c in range(4):
                    kb.mm(v4(Y.ap)[:, c, :], cur[:, c, 0, :], cur[:, c, 1, :], True, True, [cur], [Y], sig=(c == 3))
            yield
            kb.copy("act", nxt[:, :, 0, :], v4(X.ap), [X], [nxt])
            if not last:
                kb.copy("act", nxt[:, :, 1, :], v4(Y.ap), [Y], [nxt])
            yield
            for c in range(4):
                kb.mm(v4(Z.ap)[:, c, :], nxt[:, c, 0, :], PM[:, cs + c, :], True, True, [nxt, PM], [Z], sig=(c == 3))
            kb.tt("dve", PM[:, cs:cs + 4, :], PM[:, cs:cs + 4, :], v4(Z.ap), ALU.add, [PM, Z], [PM])
            yield

    def merge(gens):
        gens = list(gens)
        while gens:
            for g_ in list(gens):
                try:
                    next(g_)
                    yield
                except StopIteration:
                    gens.remove(g_)

    def stageC(half, h, SA_, CS_):
        QS, KT, VT, ZS = SA_["QS"], SA_["KT"], SA_["VT"], SA_["ZS"]
        QD, PM, AT = CS_["QD"], CS_["PM"], CS_["AT"]
        OA = gdn_state["OA"]
        SQc, RNc = SQ2, RN2
        C0 = half * 8
        pv = pbb(7).rearrange("p (a b) -> p a b", a=8)
        for c in range(8):
            kb.tr(pv[:, c, :], VT[:, c * 128:(c + 1) * 128], ident_b.ap, [VT, ident_b], [PB[7]], sig=(c == 7))
        kb.tt("dve", BV.ap, pv, beta[:, C0:C0 + 8, h].unsqueeze(2).to_broadcast([128, 8, 128]), ALU.mult,
              [PB[7], beta], [BV])
        yield
        for c in range(8):
            kb.tr(pv[:, c, :], KT[:, c * 128:(c + 1) * 128], ident_b.ap, [KT, ident_b], [PB[7]], sig=(c == 7))
        kb.tt("dve", KD.ap, pv, dkd[:, C0:C0 + 8, h].unsqueeze(2).to_broadcast([128, 8, 128]), ALU.mult,
              [PB[7], dkd], [KD])
        yield
        S_ = SST[:, h, :]
        kb.copy("act", SB_.ap, S_, [SST], [SB_])
        p3 = PB[7]
        for c in range(8):
            CG = C0 + c
            ks = slice(c * 128, (c + 1) * 128)
            col = slice(CG * NH + h, CG * NH + h + 1)
            kb.mm(p3[:, 0:128], KT[:, ks], SB_.ap, True, True, [KT, SB_], [p3])
            kb.stt("dve", ZB.ap, p3[:, 0:128], f2(nbg)[:, col], BV[:, c, :], ALU.mult, ALU.add, [p3, nbg, BV], [ZB])
            yield
            kb.mm(p3[:, 128:256], PM[:, c, :], ZB.ap, True, True, [PM, ZB], [p3])
            kb.copy("act", VN.ap, p3[:, 128:256], [p3], [VN])
            yield
            if half == 1:
                kb.mm(p3[:, 256:384], SB_.ap, QD[:, ks], True, False, [SB_, QD], [p3], sig=False)
                kb.mm(p3[:, 256:384], VN.ap, AT[:, c, :], False, True, [VN, AT], [p3])
                kb.copy("act", OT[:, ks], p3[:, 256:384], [p3], [OT])
            kb.mm(p3[:, 384:512], KD[:, c, :], VN.ap, True, True, [KD, VN], [p3])
            kb.stt("dve", SB_.ap, S_, f2(egl)[:, col], p3[:, 384:512], ALU.mult, ALU.add, [SST, egl, p3], [SB_])
            kb.stt("dve", S_, S_, f2(egl)[:, col], p3[:, 384:512], ALU.mult, ALU.add, [SST, egl, p3], [SST])
            yield
        if half == 1:
            kb.act(SQc.ap, OT.ap, AF.Square, [OT], [SQc])
            yield
            for tt in range(2):
                tsl = slice(tt * 512, (tt + 1) * 512)
                kb.mm(PB[7].ap, ones_b.ap, SQc[:, tsl], True, True, [ones_b, SQc], [PB[7]])
                kb.act(RNc[:, tsl], PB[7].ap, AF.Sqrt, [PB[7], eps_t], [RNc], bias=eps_t[:, 0:1], scale=1.0 / 128)
            yield
            kb.recip(RNc.ap, RNc.ap, [RNc], [RNc])
            kb.tt("dve", OT.ap, OT.ap, RNc.ap, ALU.mult, [OT, RNc], [OT])
            yield
            kb.stt("dve", OA[:, h, :], OT.ap, gng[:, 0:1], ZS.ap, ALU.mult, ALU.mult, [OT, gng, ZS], [OA])
            yield

    def run_pair(ga, gb, ra, rb):
        acc = 0.0
        da = db = False
        while not (da and db):
            if not db:
                try:
                    next(gb)
                except StopIteration:
                    db = True
            acc += ra / rb
            while acc >= 1.0 or (db and not da):
                acc -= 1.0
                if da:
                    acc = 0.0
                    break
                try:
                    next(ga)
                except StopIteration:
                    da = True
                    break

    def empty():
        return
        yield

    tasks = [(half, h) for half in range(2) for h in range(NH)]
    if ntasks is not None:
        tasks = tasks[:ntasks]
    if tasklist is not None:
        tasks = list(tasklist)
        ntasks = len(tasks)

    def chain(*gs):
        for g_ in gs:
            yield from g_

    def stream1(i):
        half, h = tasks[i]
        slot, WH = pf[i]
        SA_, CS_ = SETA[i % 2], CSET[i % 2]
        return chain(stageA(half, h, slot, WH, SA_, [("proj", 0), ("proj", 1), ("post", 0), ("proj", 2), ("post", 1), ("norm", 0), ("norm", 1)]),
                     merge([stageA(half, h, slot, WH, SA_, [("proj", 3), ("post", 2)]),
                            merge([stageB_batch(half, h, 0, SA_, CS_), stageB_batch(half, h, 1, SA_, CS_)])]))

    pf = {0: prefetch_head(*tasks[0])}
    for i in range(len(tasks) + 1):
        if i + 1 < len(tasks):
            pf[i + 1] = prefetch_head(*tasks[i + 1])
        if (ntasks is None and i == NH) or (ntasks is not None and i == 1):
            kb.free(XNP)
            gdn_state["OA"] = kb.sb("OA", [NH, TOK], BF16)
        ga = stream1(i) if i < len(tasks) else empty()
        gb = stageC(*tasks[i - 1], SETA[(i - 1) % 2], CSET[(i - 1) % 2]) if i >= 1 else empty()
        run_pair(gb, ga, 45.0, 100.0)
    OA = gdn_state["OA"]
    if stop_after_gdn:
        kb.drain()
        return kb
    kb.free(*RAWS, *CVS, *SQS, RN, SQ2, BV, KD, SB_, ZB, VN, OT)
    for s_ in SETA + CSET:
        kb.free(*s_.values())
    for s_ in BSET:
        kb.free(s_["GBC"], s_["ET"], s_["RG"], s_["ATN"], *s_["NM"])
    kb.free(SST, HALO, beta, nb, gtok, nbg, dkd, egl)
    inherit(PS, PB)
    WR.append(kb.sb("wr2", [8192], BF16))
    kb.dump("OA", OA, [NH, TOK], BF16)

    OB = kb.sb("OB", [KC, TOK], BF16)
    HS = kb.sb("HS", [2 + 1024], F32)
    CH = kb.sb("CH", [2 + 1024], F32)
    YC = kb.sb("YC", [1024], F32)
    for cc in range(KC):
        slot = wslot()
        W3 = slot.ap[:, 0:KC * 384].rearrange("p (k m n) -> p k m n", k=KC, m=3)
        for m, cbase in enumerate([C_SB, C_SC, C_SH]):
            wdma(W3[:, :, m, :], w_in_v[:, :, cbase + cc * 128:cbase + (cc + 1) * 128], slot)
        ph = psf(3)
        for m in (2, 1):
            for kc in range(KC):
                kb.mm(ph[:, (m - 1) * 2:(m - 1) * 2 + 2], W3[:, kc, m, :], XH[:, kc, :], kc == 0, kc == KC - 1,
                      [slot, XH], [PS[3]], sig=(kc == KC - 1))
        for m in (2, 1, 0):
            pacc = psf(m)
            for tt in range(2):
                for kc in range(KC):
                    kb.mm(pacc[:, tt * 512:(tt + 1) * 512], W3[:, kc, m, :], XN[:, kc, tt * 512:(tt + 1) * 512],
                          kc == 0, kc == KC - 1, [slot, XN], [PS[m]], sig=(kc == KC - 1))
        kb.copy("act", HS[:, 2:1026], psf(2), [PS[2]], [HS])
        kb.copy("act", HS[:, 0:2], ph[:, 2:4], [PS[3]], [HS])
        kb.tt("dve", CH[:, 2:1026], psf(1), HS[:, 2:1026], ALU.mult, [PS[1], HS], [CH])
        kb.tt("dve", CH[:, 0:2], ph[:, 0:2], HS[:, 0:2], ALU.mult, [PS[3], HS], [CH])
        wc = lambda i: cwsc[:, cc, i:i + 1]
        kb.ts("dve", YC.ap, CH[:, 0:1024], wc(0), None, ALU.mult, None, [CH, cwsc], [YC])
        kb.stt("dve", YC.ap, CH[:, 1:1025], wc(1), YC.ap, ALU.mult, ALU.add, [CH, cwsc, YC], [YC])
        kb.stt("dve", YC.ap, CH[:, 2:1026], wc(2), YC.ap, ALU.mult, ALU.add, [CH, cwsc, YC], [YC])
        kb.tt("dve", OB[:, cc, :], psf(0), YC.ap, ALU.mult, [PS[0], YC], [OB])
    kb.free(HS, CH, YC, XH)
    kb.dump("OB", OB, [KC, TOK], BF16)

    MG = kb.sb("MG", [KC, TOK], BF16)
    SA = [kb.sb(f"sa{i}", [512], F32) for i in range(2)]
    SBG = [kb.sb(f"sbg{i}", [512], F32) for i in range(2)]
    w_pa_v = w_pa.rearrange("(k p) n -> p k n", p=128)
    w_pb_v = w_pb.rearrange("(k p) n -> p k n", p=128)
    it = 0
    for cc in range(KC):
        slot = wslot()
        W4 = slot.ap.rearrange("p (k m n) -> p k m n", k=KC, m=4)
        cs_ = slice(cc * 128, (cc + 1) * 128)
        wdma(W4[:, :, 0, :], w_in_v[:, :, C_GA + cc * 128:C_GA + (cc + 1) * 128], slot)
        wdma(W4[:, :, 1, :], w_in_v[:, :, C_GB + cc * 128:C_GB + (cc + 1) * 128], slot)
        wdma(W4[:, :, 2, :], w_pa_v[:, :, cs_], slot)
        wdma(W4[:, :, 3, :], w_pb_v[:, :, cs_], slot)
        for tt in range(2):
            tsl = slice(tt * 512, (tt + 1) * 512)
            pi = (it % 2) * 2
            it += 1
            P01, P23 = PS[pi], PS[pi + 1]
            accs = [psf(pi)[:, 0:512], psf(pi)[:, 512:1024], psf(pi + 1)[:, 0:512], psf(pi + 1)[:, 512:1024]]
            srcs = [XN, XN, OA, OB]
            for m in range(4):
                PB_ = P01 if m < 2 else P23
                for kc in range(KC):
                    kb.mm(accs[m], W4[:, kc, m, :], srcs[m][:, kc, tsl], kc == 0, kc == KC - 1,
                          [slot, srcs[m]], [PB_], sig=(kc == KC - 1))
            sa, sbg = SA[it % 2], SBG[it % 2]
            kb.act(sa.ap, accs[0], AF.Sigmoid, [P01], [sa])
            kb.act(sbg.ap, accs[1], AF.Sigmoid, [P01], [sbg])
            kb.tt("dve", sa.ap, accs[2], sa.ap, ALU.mult, [P23, sa], [sa])
            kb.tt("dve", sbg.ap, accs[3], sbg.ap, ALU.mult, [P23, sbg], [sbg])
            kb.tt("dve", MG[:, cc, tsl], sa.ap, sbg.ap, ALU.add, [sa, sbg], [MG])
    kb.free(*SA, *SBG, OA, OB, XN)
    kb.dump("MG", MG, [KC, TOK], BF16)

    HT = kb.sb("HT", [KC, TOK], F32)
    XBL = [kb.sb(f"xbl{i}", [8, 128], F32) for i in range(2)]
    w_out_v = w_out.rearrange("(k p) n -> p k n", p=128)
    x_own_v = xg[1024:2048, :].rearrange("(t p) f -> p t f", p=128)
    for c4 in range(4):
        slot = wslot()
        WO = slot.ap.rearrange("p (k n) -> p k n", k=KC)
        wdma(WO, w_out_v[:, :, c4 * 512:(c4 + 1) * 512], slot)
        for cl in range(4):
            cc = c4 * 4 + cl
            xb = XBL[cc % 2]
            kb.dma("sp", xb.ap, x_own_v[:, :, cc * 128:(cc + 1) * 128], W=[xb])
            for tt in range(2):
                pi = (cc * 2 + tt) % 4
                acc = psf(pi)[:, 0:512]
                for kc in range(KC):
                    kb.mm(acc, WO[:, kc, cl * 128:(cl + 1) * 128], MG[:, kc, tt * 512:(tt + 1) * 512], kc == 0, False,
                          [slot, MG], [PS[pi]], sig=False)
                for j in range(4):
                    kb.mm(acc[:, j * 128:(j + 1) * 128], xb[:, tt * 4 + j, :], ident_f.ap, False, j == 3,
                          [xb, ident_f], [PS[pi]], sig=(j == 3))
                kb.copy("act" if tt == 0 else "dve", HT[:, cc, tt * 512:(tt + 1) * 512], acc, [PS[pi]], [HT])
    kb.free(MG, *XBL)
    kb.dump("HT1", HT, [KC, TOK], F32)

    def fm_rstd(RS):
        SQn = [kb.sb(f"sqn{i}", [1024], BF16) for i in range(2)]
        pn = psf(0)
        for kc in range(KC):
            q_ = SQn[kc % 2]
            kb.act(q_.ap, HT[:, kc, :], AF.Square, [HT], [q_])
            for tt in range(2):
                kb.mm(pn[:, tt * 512:(tt + 1) * 512], ones_b.ap, q_[:, tt * 512:(tt + 1) * 512], kc == 0, kc == KC - 1,
                      [ones_b, q_], [PS[0]], sig=(kc == KC - 1 or tt == 1))
        kb.act(RS.ap, pn, AF.Sqrt, [PS[0], eps_t], [RS], bias=eps_t[:, 0:1], scale=1.0 / D)
        kb.recip(RS.ap, RS.ap, [RS], [RS])
        kb.free(*SQn)

    RS = kb.sb("RS", [1024], F32)
    fm_rstd(RS)
    HN = kb.sb("HN", [KC, TOK], BF16)
    for kc in range(KC):
        kb.stt("dve", HN[:, kc, :], HT[:, kc, :], gffn[:, kc:kc + 1], RS.ap, ALU.mult, ALU.mult, [HT, gffn, RS], [HN])
    kb.dump("HN", HN, [KC, TOK], BF16)
    NG_ = 4
    GF = NFC // NG_
    FF = [kb.sb(f"ff{i}", [GF, TOK], BF16) for i in range(2)]
    SG = [kb.sb(f"sg{i}", [1024], F32) for i in range(2)]
    w_gate_v = w_gate.rearrange("(k p) n -> p k n", p=128)
    w_up_v = w_up.rearrange("(k p) n -> p k n", p=128)

    def down_group(g):
        ffb = FF[g % 2]
        for ccp in range(8):
            slot = wslot()
            WD = slot.ap[:, 0:GF * 256].rearrange("p (f n) -> p f n", f=GF)
            r0 = g * GF * 128
            wdma(WD, w_down[r0:r0 + GF * 128, ccp * 256:(ccp + 1) * 256].rearrange("(f p) n -> p f n", p=128), slot)
            for cl in range(2):
                cc = ccp * 2 + cl
                for tt in range(2):
                    pi = (cc * 2 + tt) % 4
                    acc = psf(pi)[:, 0:512]
                    for fcl in range(GF):
                        kb.mm(acc, WD[:, fcl, cl * 128:(cl + 1) * 128], ffb[:, fcl, tt * 512:(tt + 1) * 512],
                              fcl == 0, fcl == GF - 1, [slot, ffb], [PS[pi]], sig=(fcl == GF - 1))
                    hsl = HT[:, cc, tt * 512:(tt + 1) * 512]
                    kb.tt("dve", hsl, hsl, acc, ALU.add, [HT, PS[pi]], [HT])

    pending_groups = []
    for f2_ in range(NFC // 2):
        slot = wslot()
        WG = slot.ap.rearrange("p (k m n) -> p k m n", k=KC, m=2)
        c0 = f2_ * 256
        wdma(WG[:, :, 0, :], w_gate_v[:, :, c0:c0 + 256], slot)
        wdma(WG[:, :, 1, :], w_up_v[:, :, c0:c0 + 256], slot)
        for fl in range(2):
            fc = f2_ * 2 + fl
            g, fcl = fc // GF, fc % GF
            ffb = FF[g % 2]
            pg_i, pu_i = (0, 1) if fl == 0 else (2, 3)
            for m, pi in ((0, pg_i), (1, pu_i)):
                for tt in range(2):
                    for kc in range(KC):
                        kb.mm(psf(pi)[:, tt * 512:(tt + 1) * 512], WG[:, kc, m, fl * 128:(fl + 1) * 128],
                              HN[:, kc, tt * 512:(tt + 1) * 512], kc == 0, kc == KC - 1, [slot, HN], [PS[pi]],
                              sig=(kc == KC - 1))
            sg = SG[fl]
            kb.act(sg.ap, psf(pg_i), AF.Silu, [PS[pg_i]], [sg])
            kb.tt("dve", ffb[:, fcl, :], psf(pu_i), sg.ap, ALU.mult, [PS[pu_i], sg], [ffb])
            if fcl == GF - 1:
                pending_groups.append(g)
        for g in pending_groups:
            down_group(g)
        pending_groups = []
    kb.free(HN, *FF, *SG)
    kb.dump("HT2", HT, [KC, TOK], F32)

    fm_rstd(RS)
    for kc in range(KC):
        kb.stt("dve", HT[:, kc, :], HT[:, kc, :], gfin[:, kc:kc + 1], RS.ap, ALU.mult, ALU.mult, [HT, gfin, RS], [HT])
    YO = [kb.sb(f"yo{i}", [D], F32) for i in range(2)]
    k_ = 0
    for t in range(8):
        yo = YO[t % 2]
        for g4 in range(4):
            pi = k_ % 4
            k_ += 1
            pt = psf(pi)[:, 0:512].rearrange("p (a b) -> p a b", a=4)
            for j in range(4):
                kc = g4 * 4 + j
                kb.tr(pt[:, j, :], HT[:, kc, t * 128:(t + 1) * 128], ident_f.ap, [HT, ident_f], [PS[pi]], sig=(j == 3))
            kb.copy("act" if g4 % 2 == 0 else "dve", yo[:, g4 * 512:(g4 + 1) * 512],
                    psf(pi)[:, 0:512], [PS[pi]], [yo])
        kb.dma("sp", y[t * 128:(t + 1) * 128, :], yo.ap, R=[yo])
    kb.drain()
    return kb


_CACHE = {}


def _prep_inputs(inp):
    f = lambda a: np.ascontiguousarray(np.asarray(a, dtype=np.float32))
    x = f(inp["x"])
    L = 0
    pcol = lambda v: np.ascontiguousarray(v.reshape(-1, 128).T)
    shared = {
        "w_in": f(inp["w_in"][L]), "w_proj_a": f(inp["w_proj_a"][L]), "w_proj_b": f(inp["w_proj_b"][L]),
        "w_out": f(inp["w_out"][L]), "w_gate": f(inp["w_gate"][L]), "w_up": f(inp["w_up"][L]),
        "w_down": f(inp["w_down"][L]),
        "gmix_row": np.ascontiguousarray(np.broadcast_to(f(inp["ln_mix_g"][L])[None, :], (128, D))),
        "gffn_p": pcol(f(inp["ln_ffn_g"][L])),
        "gfin_p": pcol(f(inp["ln_final_g"])),
        "cw_qkv": np.ascontiguousarray(f(inp["conv_qkv_w"][L]).T.reshape(48, 128, 4).transpose(1, 0, 2)),
        "cw_sc": np.ascontiguousarray(f(inp["conv_sc_w"][L]).T.reshape(KC, 128, 3).transpose(1, 0, 2)),
        "alog_row": np.ascontiguousarray(np.broadcast_to(f(inp["A_log"][L])[None, :], (128, NH))),
        "dtb_row": np.ascontiguousarray(np.broadcast_to(f(inp["dt_bias"][L])[None, :], (128, NH))),
        "gng_p": np.ascontiguousarray(f(inp["gdn_norm_g"][L]).reshape(128, 1)),
    }
    in_maps = []
    zeros = np.zeros((1024, D), np.float32)
    for c in range(8):
        b, s = c // 2, c % 2
        if s == 0:
            xgc = np.concatenate([zeros, x[b, :1024]], axis=0)
        else:
            xgc = x[b]
        m = dict(shared)
        m["xg"] = np.ascontiguousarray(xgc)
        in_maps.append(m)
    return in_maps


def kernel(**inputs):
    if "kb" not in _CACHE:
        _CACHE["kb"] = build()
    kb = _CACHE["kb"]
    in_maps = _prep_inputs(inputs)
    res = run_bass_kernel_spmd(kb.nc, in_maps, core_ids=list(range(8)))
    out = np.empty((4, 2048, D), np.float32)
    for c in range(8):
        b, s = c // 2, c % 2
        out[b, s * 1024:(s + 1) * 1024] = res.results[c]["y"]
    return out
```

```python
import numpy as np
from contextlib import ExitStack
import concourse.bass as bass
import concourse.mybir as mybir
from concourse.bass_utils import run_bass_kernel_spmd

F32 = mybir.dt.float32
BF16 = mybir.dt.bfloat16
AF = mybir.ActivationFunctionType
ALU = mybir.AluOpType

D = 2048
KC = 16
NH = 16
TOK = 1024
NG = 2048
DFF = 5632
NFC = 44
INW = 18464
C_Q, C_K, C_V, C_Z, C_B, C_A = 0, 2048, 4096, 6144, 8192, 8208
C_SB, C_SC, C_SH, C_GA, C_GB = 8224, 10272, 12320, 14368, 16416
EPS = 1e-6


class Buf:
    def __init__(self, ap, name=""):
        self.ap = ap
        self.name = name
        self.w = {}
        self.r = {}

    def __getitem__(self, k):
        return self.ap[k]


class Arena:
    def __init__(self, cap):
        self.cap = cap
        self.free_list = [(0, cap)]
        self.live = {}
        self.dead = []

    def alloc(self, n):
        n = (n + 63) // 64 * 64
        for i, (o, s) in enumerate(self.free_list):
            if s >= n:
                if s == n:
                    self.free_list.pop(i)
                else:
                    self.free_list[i] = (o + n, s - n)
                return o, n
        raise MemoryError(f"arena: cannot alloc {n}, free={self.free_list}")

    def release(self, off, n):
        self.free_list.append((off, n))
        self.free_list.sort()
        out = []
        for o, s in self.free_list:
            if out and out[-1][0] + out[-1][1] == o:
                out[-1] = (out[-1][0], out[-1][1] + s)
            else:
                out.append((o, s))
        self.free_list = out


class KB:
    def __init__(self, debug=()):
        self.debug = set(debug)
        self.dbg_out = {}
        self.nc = nc = bass.Bass("TRN2", target_bir_lowering=False)
        self.eng = {"pe": nc.tensor, "act": nc.scalar, "dve": nc.vector, "pool": nc.gpsimd, "sp": nc.sync}
        self.sem = {}
        self.cnt = {}
        for e in ["pe", "act", "dve", "pool"]:
            self.sem[e] = nc.alloc_semaphore("s_" + e)
            self.cnt[e] = 0
        self.dq = {}
        for q in ["sp", "pool"]:
            sems = [nc.alloc_semaphore(f"s_dma_{q}{i}") for i in range(6)]
            self.dq[q] = {"sems": sems, "cnt": [0] * 6, "next": 0}
        self.seen = {}
        self.pe_pending = []
        cap = (nc.sbuf_bytes_remaining - 256) // 64 * 64
        self.arena_t = nc.alloc_sbuf_tensor("arena", [128, cap // 2], BF16)
        self.arena = Arena(cap)
        self.nwaits = 0

    def sb(self, name, shape, dtype):
        esz = 4 if dtype == F32 else 2
        n = int(np.prod(shape)) * esz
        off, n2 = self.arena.alloc(n)
        v = self.arena_t[:, off // 2:(off + n) // 2]
        if dtype == F32:
            v = v.bitcast(F32)
        if len(shape) == 2:
            v = v.rearrange("p (a b) -> p a b", a=shape[0])
        elif len(shape) == 3:
            v = v.rearrange("p (a b c) -> p a b c", a=shape[0], b=shape[1])
        elif len(shape) == 4:
            v = v.rearrange("p (a b c d) -> p a b c d", a=shape[0], b=shape[1], c=shape[2])
        b = Buf(v, name)
        b.off, b.n = off, n2
        keep = []
        for (o, s, ob) in self.arena.dead:
            if o < off + n2 and off < o + s:
                for k, val in ob.w.items():
                    b.r[k] = max(b.r.get(k, 0), val)
                for k, val in ob.r.items():
                    b.r[k] = max(b.r.get(k, 0), val)
                keep.append((o, s, ob))
            else:
                keep.append((o, s, ob))
        self.arena.dead = keep
        return b

    def free(self, *bufs):
        for b in bufs:
            self.arena.release(b.off, b.n)
            self.arena.dead.append((b.off, b.n, b))

    def _semof(self, key):
        if isinstance(key, str):
            return self.sem[key]
        q, i = key
        return self.dq[q]["sems"][i]

    def _wait(self, e, key, val):
        if e == "pe" and key == "pe":
            return
        seen = self.seen.setdefault(e, {})
        if seen.get(key, 0) >= val:
            return
        seen[key] = val
        self.eng[e].wait_ge(self._semof(key), val)
        self.nwaits += 1

    def _deps(self, e, R, W):
        for b in R:
            for k, v in b.w.items():
                self._wait(e, k, v)
            if getattr(b, "psum", False):
                for k, v in b.r.items():
                    if k != e:
                        self._wait(e, k, v)
        for b in W:
            for k, v in b.w.items():
                self._wait(e, k, v)
            for k, v in b.r.items():
                self._wait(e, k, v)

    def _mark(self, key, val, R, W):
        for b in R:
            b.r[key] = max(b.r.get(key, 0), val)
        for b in W:
            b.w[key] = max(b.w.get(key, 0), val)
            b.r = {}

    def op(self, e, fn, R=(), W=(), sig=True):
        self._deps(e, R, W)
        ins = fn()
        if e == "pe" and not sig:
            self.pe_pending.append((R, W))
            return
        self.cnt[e] += 1
        ins.then_inc(self.sem[e], 1)
        if e == "pe" and self.pe_pending:
            for (r2, w2) in self.pe_pending:
                self._mark(e, self.cnt[e], r2, w2)
            self.pe_pending = []
        self._mark(e, self.cnt[e], R, W)

    def dma(self, q, out, in_, R=(), W=()):
        dq = self.dq[q]
        i = dq["next"]
        dq["next"] = (i + 1) % len(dq["sems"])
        if dq["cnt"][i] > 0:
            self._wait(q, (q, i), dq["cnt"][i])
        self._deps(q, R, W)
        self.eng[q].dma_start(out=out, in_=in_).then_inc(dq["sems"][i], 16)
        dq["cnt"][i] += 16
        self._mark((q, i), dq["cnt"][i], R, W)

    def drain(self):
        for q in self.dq:
            for i, c in enumerate(self.dq[q]["cnt"]):
                if c:
                    self._wait("sp", (q, i), c)
        for e in ["pe", "act", "dve", "pool"]:
            if self.cnt[e]:
                self._wait("sp", e, self.cnt[e])

    def mm(self, out, lhsT, rhs, start, stop, R, W, sig=True):
        nc = self.nc
        self.op("pe", lambda: nc.tensor.matmul(out, lhsT=lhsT, rhs=rhs, start=start, stop=stop), R, W, sig)

    def tr(self, out, in_, ident, R, W, sig=True):
        nc = self.nc
        self.op("pe", lambda: nc.tensor.transpose(out=out, in_=in_, identity=ident), R, W, sig)

    def act(self, out, in_, func, R, W, bias=None, scale=None, accum_out=None):
        nc = self.nc
        kw = {}
        if bias is not None:
            kw["bias"] = bias
        if scale is not None:
            kw["scale"] = scale
        if accum_out is not None:
            kw["accum_out"] = accum_out
        self.op("act", lambda: nc.scalar.activation(out=out, in_=in_, func=func, **kw), R, W)

    def tt(self, e, out, in0, in1, op, R, W):
        E = self.eng[e]
        self.op(e, lambda: E.tensor_tensor(out=out, in0=in0, in1=in1, op=op), R, W)

    def ts(self, e, out, in0, s1, s2, op0, op1, R, W):
        E = self.eng[e]
        if op1 is None:
            self.op(e, lambda: E.tensor_scalar(out=out, in0=in0, scalar1=s1, scalar2=None, op0=op0), R, W)
        else:
            self.op(e, lambda: E.tensor_scalar(out=out, in0=in0, scalar1=s1, scalar2=s2, op0=op0, op1=op1), R, W)

    def stt(self, e, out, in0, scalar, in1, op0, op1, R, W):
        E = self.eng[e]
        self.op(e, lambda: E.scalar_tensor_tensor(out=out, in0=in0, scalar=scalar, in1=in1, op0=op0, op1=op1), R, W)

    def copy(self, e, out, in_, R, W):
        if e == "act":
            self.act(out, in_, AF.Copy, R, W)
        else:
            E = self.eng[e]
            self.op(e, lambda: E.tensor_copy(out=out, in_=in_), R, W)

    def memset(self, e, ap, val, W):
        E = self.eng[e]
        self.op(e, lambda: E.memset(ap, val), (), W)

    def recip(self, out, in_, R, W):
        nc = self.nc
        self.op("dve", lambda: nc.vector.reciprocal(out=out, in_=in_), R, W)

    def dump(self, name, buf, shape, dtype=F32):
        if name not in self.debug:
            return
        t = self.nc.dram_tensor("dbg_" + name, [128] + list(shape), dtype, kind="ExternalOutput").ap()
        self.dbg_out[name] = "dbg_" + name
        self.dma("sp", t, buf.ap, R=[buf])


def build(debug=(), ntasks=None, stop_after_gdn=False, tasklist=None):
    kb = KB(debug)
    nc = kb.nc
    DI = lambda n, s: nc.dram_tensor(n, s, F32, kind="ExternalInput").ap()
    xg = DI("xg", [NG, D])
    w_in = DI("w_in", [D, INW])
    w_pa = DI("w_proj_a", [D, D])
    w_pb = DI("w_proj_b", [D, D])
    w_out = DI("w_out", [D, D])
    w_gate = DI("w_gate", [D, DFF])
    w_up = DI("w_up", [D, DFF])
    w_down = DI("w_down", [DFF, D])
    gmix_row_d = DI("gmix_row", [128, D])
    gffn_d = DI("gffn_p", [128, KC])
    gfin_d = DI("gfin_p", [128, KC])
    cwqkv_d = DI("cw_qkv", [128, 48, 4])
    cwsc_d = DI("cw_sc", [128, KC, 3])
    alog_d = DI("alog_row", [128, NH])
    dtb_d = DI("dtb_row", [128, NH])
    gng_d = DI("gng_p", [128, 1])
    y = nc.dram_tensor("y", [TOK, D], F32, kind="ExternalOutput").ap()

    PS = []
    for i in range(4):
        t = nc.alloc_psum_tensor(f"ps{i}", [128, 1024], F32)
        PS.append(Buf(t[:, :], f"ps{i}"))
        PS[-1].psum = True

    def psf(i, shape=None):
        return PS[i].ap

    def psb(i):
        return PS[i].ap.bitcast(BF16)

    ident_b = kb.sb("ident_b", [128], BF16)
    ident_f = kb.sb("ident_f", [128], F32)
    ones_b = kb.sb("ones_b", [128], BF16)
    ones_f = kb.sb("ones_f", [128], F32)
    tri_f = kb.sb("tri_f", [128], F32)
    sl_f = kb.sb("sl_f", [128], F32)
    li_f = kb.sb("li_f", [128], F32)
    eps_t = kb.sb("eps_t", [1], F32)
    gffn = kb.sb("gffn", [KC], F32)
    gfin = kb.sb("gfin", [KC], F32)
    cwqkv = kb.sb("cwqkv", [48, 4], F32)
    cwsc = kb.sb("cwsc", [KC, 3], F32)
    negA = kb.sb("negA", [NH], F32)
    dtb = kb.sb("dtb", [NH], F32)
    gng = kb.sb("gng", [1], F32)

    def sel(buf, pattern, cm, cmp):
        kb.memset("pool", buf.ap, 1.0, [buf])
        kb.op("pool", lambda: nc.gpsimd.affine_select(out=buf.ap, in_=buf.ap, pattern=pattern, compare_op=cmp,
                                                      fill=0.0, base=0, channel_multiplier=cm), [buf], [buf])

    sel(ident_f, [[-1, 128]], 1, ALU.is_equal)
    sel(tri_f, [[1, 128]], -1, ALU.is_ge)
    sel(sl_f, [[-1, 128]], 1, ALU.is_gt)
    sel(li_f, [[-1, 128]], 1, ALU.is_ge)
    kb.memset("pool", ones_f.ap, 1.0, [ones_f])
    kb.memset("pool", ones_b.ap, 1.0, [ones_b])
    kb.memset("pool", eps_t.ap, EPS, [eps_t])
    kb.copy("dve", ident_b.ap, ident_f.ap, [ident_f], [ident_b])
    kb.dma("sp", gffn.ap, gffn_d, W=[gffn])
    kb.dma("sp", gfin.ap, gfin_d, W=[gfin])
    kb.dma("sp", cwqkv.ap, cwqkv_d, W=[cwqkv])
    kb.dma("sp", cwsc.ap, cwsc_d, W=[cwsc])
    kb.dma("sp", negA.ap, alog_d, W=[negA])
    kb.dma("sp", dtb.ap, dtb_d, W=[dtb])
    kb.dma("sp", gng.ap, gng_d, W=[gng])
    kb.act(negA.ap, negA.ap, AF.Exp, [negA], [negA])
    kb.ts("dve", negA.ap, negA.ap, -1.0, None, ALU.mult, None, [negA], [negA])

    WR = [kb.sb(f"wr{i}", [8192], BF16) for i in range(2)]
    wr_state = {"i": 0}

    def wslot():
        b = WR[wr_state["i"] % len(WR)]
        wr_state["i"] += 1
        return b

    def wdma(dst, src, slot):
        kb.dma("pool", dst, src, W=[slot])

    w_in_v = w_in.rearrange("(k p) n -> p k n", p=128)

    XN = kb.sb("XN", [KC, TOK], BF16)
    XNP = kb.sb("XNP", [KC, TOK], BF16)
    gmix = kb.sb("gmix", [D], F32)
    kb.dma("sp", gmix.ap, gmix_row_d, W=[gmix])
    xt = [kb.sb(f"xt{i}", [D], F32) for i in range(2)]
    xs = [kb.sb(f"xs{i}", [D], BF16) for i in range(2)]
    junk = kb.sb("junk", [D], BF16)
    ssq = kb.sb("ssq", [16, 2], F32)

    for t in range(16):
        X_t, S_t = xt[t % 2], xs[t % 2]
        dst = XNP if t < 8 else XN
        tl = t % 8
        kb.dma("sp", X_t.ap, xg[t * 128:(t + 1) * 128, :], W=[X_t])
        kb.act(junk.ap, X_t.ap, AF.Square, [X_t], [junk, ssq], accum_out=ssq[:, t, 0:1])
        kb.act(ssq[:, t, 1:2], ssq[:, t, 0:1], AF.Sqrt, [ssq, eps_t], [ssq], bias=eps_t[:, 0:1], scale=1.0 / D)
        kb.recip(ssq[:, t, 1:2], ssq[:, t, 1:2], [ssq], [ssq])
        kb.stt("dve", S_t.ap, X_t.ap, ssq[:, t, 1:2], gmix.ap, ALU.mult, ALU.mult, [X_t, ssq, gmix], [S_t])
        for g in range(2):
            pb = PS[(2 * t + g) % 4]
            pv = psb((2 * t + g) % 4)[:, 0:1024].rearrange("p (a b) -> p a b", a=8)
            for j in range(8):
                kc = g * 8 + j
                kb.tr(pv[:, j, :], S_t[:, kc * 128:(kc + 1) * 128], ident_b.ap, [S_t, ident_b], [pb], sig=(j == 7))
            kb.copy("act" if g == 0 else "dve", dst[:, g * 8:(g + 1) * 8, tl * 128:(tl + 1) * 128], pv, [pb], [dst])
    kb.free(gmix, junk, ssq, *xt, *xs)
    kb.dump("XN", XN, [KC, TOK], BF16)
    kb.dump("XNP", XNP, [KC, TOK], BF16)

    XH = kb.sb("XH", [KC, 2], BF16)
    kb.copy("dve", XH.ap, XNP[:, :, 1022:1024], [XNP], [XH])

    def xn_tok(kc, t0, n):
        if t0 + n <= 1024:
            return XNP[:, kc, t0:t0 + n]
        assert t0 >= 1024
        return XN[:, kc, t0 - 1024:t0 - 1024 + n]

    def xn_buf(t0):
        return XNP if t0 < 1024 else XN

    wba_slot = wslot()
    wba = wba_slot.ap[:, 0:KC * 32].rearrange("p (k n) -> p k n", k=KC)
    wdma(wba, w_in_v[:, :, C_B:C_B + 32], wba_slot)
    ba = kb.sb("ba", [16, 32], F32)
    beta = kb.sb("beta", [16, NH], F32)
    nb = kb.sb("nb", [16, NH], F32)
    gtok = kb.sb("gtok", [16, NH], F32)
    gcum = kb.sb("gcum", [16, NH], F32)
    nbg = kb.sb("nbg", [16, NH], F32)
    dkd = kb.sb("dkd", [16, NH], F32)
    egl = kb.sb("egl", [16, NH], F32)
    tmp1 = kb.sb("tmp1", [16, NH], F32)
    tmp2 = kb.sb("tmp2", [16, NH], F32)
    pba = psf(0)[:, 0:512].rearrange("p (c n) -> p c n", c=16)
    for c in range(16):
        for kc in range(KC):
            kb.mm(pba[:, c, :], xn_tok(kc, c * 128, 128), wba[:, kc, :], kc == 0, kc == KC - 1,
                  [xn_buf(c * 128), wba_slot], [PS[0]], sig=(kc == KC - 1))
    kb.copy("act", ba.ap, pba, [PS[0]], [ba])
    kb.act(beta.ap, ba[:, :, 0:16], AF.Sigmoid, [ba], [beta])
    kb.ts("dve", nb.ap, beta.ap, -1.0, None, ALU.mult, None, [beta], [nb])
    dtb_b = dtb.ap.unsqueeze(1).to_broadcast([128, 16, NH])
    negA_b = negA.ap.unsqueeze(1).to_broadcast([128, 16, NH])
    kb.tt("dve", tmp1.ap, ba[:, :, 16:32], dtb_b, ALU.add, [ba, dtb], [tmp1])
    kb.act(tmp2.ap, tmp1.ap, AF.Abs, [tmp1], [tmp2])
    kb.act(tmp2.ap, tmp2.ap, AF.Exp, [tmp2], [tmp2], scale=-1.0)
    kb.act(tmp2.ap, tmp2.ap, AF.Ln, [tmp2], [tmp2], bias=1.0)
    kb.ts("dve", tmp1.ap, tmp1.ap, 0.0, None, ALU.max, None, [tmp1], [tmp1])
    kb.tt("dve", tmp1.ap, tmp1.ap, tmp2.ap, ALU.add, [tmp1, tmp2], [tmp1])
    kb.tt("dve", gtok.ap, tmp1.ap, negA_b, ALU.mult, [tmp1, negA], [gtok])
    gflat = gtok.ap.rearrange("p c h -> p (c h)")
    pg = psf(1)
    kb.mm(pg[:, 0:256], tri_f.ap, gflat, True, True, [tri_f, gtok], [PS[1]])
    kb.mm(pg[:, 512:768], ones_f.ap, gflat, True, True, [ones_f, gtok], [PS[1]])
    f2 = lambda b: b.ap.rearrange("p c h -> p (c h)")
    kb.copy("act", f2(gcum), pg[:, 0:256], [PS[1]], [gcum])
    kb.act(f2(egl), pg[:, 512:768], AF.Exp, [PS[1]], [egl])
    kb.tt("dve", f2(tmp1), pg[:, 512:768], f2(gcum), ALU.subtract, [PS[1], gcum], [tmp1])
    kb.act(f2(dkd), f2(tmp1), AF.Exp, [tmp1], [dkd])
    kb.act(f2(tmp2), f2(gcum), AF.Exp, [gcum], [tmp2])
    kb.tt("dve", f2(nbg), f2(nb), f2(tmp2), ALU.mult, [nb, tmp2], [nbg])
    kb.free(ba, tmp1, tmp2, gcum)
    kb.dump("beta", beta, [16, NH])
    kb.dump("gtok", gtok, [16, NH])
    kb.dump("dkd", dkd, [16, NH])
    kb.dump("nbg", nbg, [16, NH])

    def inherit(dsts, srcs):
        for dd in dsts:
            for ss in srcs:
                for k_, v_ in list(ss.w.items()) + list(ss.r.items()):
                    dd.r[k_] = max(dd.r.get(k_, 0), v_)

    PB = [Buf(PS[i // 2].ap[:, (i % 2) * 512:(i % 2 + 1) * 512], f"pb{i}") for i in range(8)]
    inherit(PB, PS)
    for b_ in PB:
        b_.psum = True

    def pbb(i):
        return PB[i].ap.bitcast(BF16)

    SST = kb.sb("SST", [NH, 128], F32)
    HALO = kb.sb("HALO", [NH, 3, 3], F32)
    kb.memset("dve", SST.ap, 0.0, [SST])
    kb.memset("dve", HALO.ap, 0.0, [HALO])
    RAWS = [kb.sb(f"raw{i}", [3 + 1024], F32) for i in range(2)]
    CVS = [kb.sb(f"cv{i}", [1024], F32) for i in range(2)]
    SQS = [kb.sb(f"sq{i}", [1024], BF16) for i in range(2)]
    RN = kb.sb("rn", [1024], F32)
    SETA = [dict(QS=kb.sb(f"qs{i}", [1024], BF16), KT=kb.sb(f"kt{i}", [1024], BF16), VT=kb.sb(f"vt{i}", [1024], BF16),
                 ZS=kb.sb(f"zs{i}", [1024], F32)) for i in range(2)]
    BV = kb.sb("bv", [8, 128], BF16)
    KD = kb.sb("kd", [8, 128], BF16)
    CSET = [dict(QD=kb.sb(f"qd{i}", [1024], BF16), PM=kb.sb(f"pm{i}", [8, 128], BF16),
                 AT=kb.sb(f"at{i}", [8, 128], BF16)) for i in range(2)]
    SQ2 = kb.sb("sq2", [1024], BF16)
    RN2 = RN
    BSET = [dict(GBC=kb.sb(f"gbc{i}", [4, 128], F32), ET=kb.sb(f"et{i}", [4, 128], F32),
                 RG=kb.sb(f"rg{i}", [4, 128], F32), ATN=kb.sb(f"atn{i}", [4, 128], BF16),
                 NM=[kb.sb(f"nm{i}_{j}", [4, 2, 128], BF16) for j in range(2)]) for i in range(2)]
    SB_ = kb.sb("sbf", [128], BF16)
    ZB = kb.sb("zb", [128], BF16)
    VN = kb.sb("vn", [128], BF16)
    OT = kb.sb("ot", [1024], F32)
    gdn_state = {"OA": None}

    def prefetch_head(half, h):
        slot = wslot()
        WH = slot.ap.rearrange("p (k m n) -> p k m n", k=KC, m=4)
        nm_ = 4 if half == 1 else 3
        for m, cbase in enumerate([C_Q, C_K, C_V, C_Z][:nm_]):
            wdma(WH[:, :, m, :], w_in_v[:, :, cbase + h * 128:cbase + (h + 1) * 128], slot)
        return slot, WH

    a_k = {"k": 0}

    def stageA(half, h, slot, WH, SA_, steps):
        QS, KT, VT, ZS = SA_["QS"], SA_["KT"], SA_["VT"], SA_["ZS"]
        T0 = half * 1024
        XB_ = xn_buf(T0)
        nm_ = 4 if half == 1 else 3
        for kind, m in steps:
            if m >= nm_:
                continue
            RAW = RAWS[m % 2]
            CV = CVS[m % 2]
            SQ = SQS[m % 2]
            if kind == "proj":
                if m < 3:
                    kb.copy("dve", RAW[:, 0:3], HALO[:, h, m, :], [HALO], [RAW])
                for tt in range(2):
                    pi = a_k["k"] % 2
                    a_k["k"] += 1
                    for kc in range(KC):
                        kb.mm(PB[pi].ap, WH[:, kc, m, :], xn_tok(kc, T0 + tt * 512, 512),
                              kc == 0, kc == KC - 1, [slot, XB_], [PB[pi]], sig=(kc == KC - 1))
                    if m == 3:
                        kb.act(ZS[:, tt * 512:(tt + 1) * 512], PB[pi].ap, AF.Silu, [PB[pi]], [ZS])
                    else:
                        kb.copy("act", RAW[:, 3 + tt * 512:3 + (tt + 1) * 512], PB[pi].ap, [PB[pi]], [RAW])
                    yield
                if m < 3 and half == 0:
                    kb.copy("dve", HALO[:, h, m, :], RAW[:, 1024:1027], [RAW], [HALO])
                continue
            if kind == "norm":
                for tt in range(2):
                    pi = a_k["k"] % 2
                    a_k["k"] += 1
                    tsl = slice(tt * 512, (tt + 1) * 512)
                    kb.mm(PB[pi].ap, ones_b.ap, SQ[:, tsl], True, True, [ones_b, SQ], [PB[pi]])
                    kb.act(RN[:, tsl], PB[pi].ap, AF.Ln, [PB[pi], eps_t], [RN], bias=eps_t[:, 0:1])
                yield
                kb.act(RN.ap, RN.ap, AF.Exp, [RN], [RN], scale=-0.5)
                if m == 0:
                    kb.stt("dve", QS.ap, CV.ap, float(128 ** -0.5), RN.ap, ALU.mult, ALU.mult, [CV, RN], [QS])
                else:
                    kb.tt("dve", KT.ap, CV.ap, RN.ap, ALU.mult, [CV, RN], [KT])
                yield
                continue
            assert kind == "post" and m < 3
            wcs = [cwqkv[:, m * 16 + h, i:i + 1] for i in range(4)]
            kb.ts("dve", CV.ap, RAW[:, 0:1024], wcs[0], None, ALU.mult, None, [RAW, cwqkv], [CV])
            for i in range(1, 4):
                kb.stt("dve", CV.ap, RAW[:, i:i + 1024], wcs[i], CV.ap, ALU.mult, ALU.add, [RAW, cwqkv, CV], [CV])
                yield
            if m == 2:
                kb.act(VT.ap, CV.ap, AF.Silu, [CV], [VT])
                yield
                continue
            kb.act(CV.ap, CV.ap, AF.Silu, [CV], [CV])
            kb.act(SQ.ap, CV.ap, AF.Square, [CV], [SQ])
            yield

    def stageB_batch(half, h, cb, SA_, CS_):
        QS, KT = SA_["QS"], SA_["KT"]
        QD, PM, AT = CS_["QD"], CS_["PM"], CS_["AT"]
        bs = BSET[cb]
        GBC, ET, RG, ATN, NM = bs["GBC"], bs["ET"], bs["RG"], bs["ATN"], bs["NM"]
        X, Y, Z = (PB[2], PB[3], PB[4]) if cb == 0 else (PB[5], PB[6], PB[4])
        C0 = half * 8
        cs = cb * 4
        CG = C0 + cs
        tsl = slice(cs * 128, (cs + 4) * 128)
        v4 = lambda ap: ap.rearrange("p (c n) -> p c n", c=4)
        gsc = gtok[:, CG:CG + 4, h].unsqueeze(2).to_broadcast([128, 4, 128])
        kb.tt("dve", RG.ap, tri_f.ap.unsqueeze(1).to_broadcast([128, 4, 128]), gsc, ALU.mult, [tri_f, gtok], [RG])
        yield
        kb.mm(X.ap, ones_f.ap, RG.ap.rearrange("p c n -> p (c n)"), True, True, [ones_f, RG], [X])
        for c in range(4):
            kb.mm(v4(Y.ap)[:, c, :], RG[:, c, :], sl_f.ap, True, True, [RG, sl_f], [Y], sig=(c == 3))
        yield
        kb.act(GBC.ap.rearrange("p c n -> p (c n)"), X.ap, AF.Exp, [X], [GBC])
        kb.act(ET.ap, v4(Y.ap), AF.Exp, [Y], [ET])
        yield
        for c in range(4):
            ks = slice((cs + c) * 128, (cs + c + 1) * 128)
            kb.mm(v4(X.ap)[:, c, :], KT[:, ks], KT[:, ks], True, True, [KT], [X], sig=(c == 3))
        for c in range(4):
            ks = slice((cs + c) * 128, (cs + c + 1) * 128)
            kb.mm(v4(Y.ap)[:, c, :], QS[:, ks], KT[:, ks], True, True, [KT, QS], [Y], sig=(c == 3))
        yield
        kb.tt("dve", QD[:, tsl], QS[:, tsl], GBC.ap.rearrange("p c n -> p (c n)"), ALU.mult, [QS, GBC], [QD])
        ETM = GBC
        kb.tt("dve", ETM.ap, ET.ap, li_f.ap.unsqueeze(1).to_broadcast([128, 4, 128]), ALU.mult, [ET, li_f, GBC], [ETM])
        yield
        kb.tt("dve", ET.ap, ET.ap, sl_f.ap.unsqueeze(1).to_broadcast([128, 4, 128]), ALU.mult, [ET, sl_f], [ET])
        kb.tt("dve", ET.ap, ET.ap, nb[:, CG:CG + 4, h].unsqueeze(2).to_broadcast([128, 4, 128]), ALU.mult,
              [ET, nb], [ET])
        yield
        kb.tt("dve", NM[0][:, :, 0, :], v4(X.ap), ET.ap, ALU.mult, [X, ET], [NM[0]])
        kb.tt("dve", ATN.ap, v4(Y.ap), ETM.ap, ALU.mult, [Y, ETM], [ATN])
        yield
        pt = Z.ap.bitcast(BF16).rearrange("p (a c n) -> p a c n", a=2, c=4)
        for c in range(4):
            kb.tr(pt[:, 0, c, :], NM[0][:, c, 0, :], ident_b.ap, [NM[0], ident_b], [Z], sig=False)
        for c in range(4):
            kb.tr(pt[:, 1, c, :], ATN[:, c, :], ident_b.ap, [ATN, ident_b], [Z], sig=(c == 3))
        kb.copy("act", NM[0][:, :, 1, :], pt[:, 0], [Z], [NM[0]])
        kb.copy("act", AT[:, cs:cs + 4, :], pt[:, 1], [Z], [AT])
        kb.tt("dve", PM[:, cs:cs + 4, :], NM[0][:, :, 1, :],
              ident_b.ap.unsqueeze(1).to_broadcast([128, 4, 128]), ALU.add, [NM[0], ident_b], [PM])
        yield
        for step in range(1, 7):
            cur, nxt = NM[(step - 1) % 2], NM[step % 2]
            last = (step == 6)
            for c in range(4):
                kb.mm(v4(X.ap)[:, c, :], cur[:, c, 1, :], cur[:, c, 0, :], True, True, [cur], [X], sig=(c == 3))
            if not last:
                for c in range(4):
                    kb.mm(v4(Y.ap)[:, c, :], cur[:, c, 0, :], cur[:, c, 1, :], True, True, [cur], [Y], sig=(c == 3))
            yield
            kb.copy("act", nxt[:, :, 0, :], v4(X.ap), [X], [nxt])
            if not last:
                kb.copy("act", nxt[:, :, 1, :], v4(Y.ap), [Y], [nxt])
            yield
            for c in range(4):
                kb.mm(v4(Z.ap)[:, c, :], nxt[:, c, 0, :], PM[:, cs + c, :], True, True, [nxt, PM], [Z], sig=(c == 3))
            kb.tt("dve", PM[:, cs:cs + 4, :], PM[:, cs:cs + 4, :], v4(Z.ap), ALU.add, [PM, Z], [PM])
            yield

    def merge(gens):
        gens = list(gens)
        while gens:
            for g_ in list(gens):
                try:
                    next(g_)
                    yield
                except StopIteration:
                    gens.remove(g_)

    def stageC(half, h, SA_, CS_):
        QS, KT, VT, ZS = SA_["QS"], SA_["KT"], SA_["VT"], SA_["ZS"]
        QD, PM, AT = CS_["QD"], CS_["PM"], CS_["AT"]
        OA = gdn_state["OA"]
        SQc, RNc = SQ2, RN2
        C0 = half * 8
        pv = pbb(7).rearrange("p (a b) -> p a b", a=8)
        for c in range(8):
            kb.tr(pv[:, c, :], VT[:, c * 128:(c + 1) * 128], ident_b.ap, [VT, ident_b], [PB[7]], sig=(c == 7))
        kb.tt("dve", BV.ap, pv, beta[:, C0:C0 + 8, h].unsqueeze(2).to_broadcast([128, 8, 128]), ALU.mult,
              [PB[7], beta], [BV])
        yield
        for c in range(8):
            kb.tr(pv[:, c, :], KT[:, c * 128:(c + 1) * 128], ident_b.ap, [KT, ident_b], [PB[7]], sig=(c == 7))
        kb.tt("dve", KD.ap, pv, dkd[:, C0:C0 + 8, h].unsqueeze(2).to_broadcast([128, 8, 128]), ALU.mult,
              [PB[7], dkd], [KD])
        yield
        S_ = SST[:, h, :]
        kb.copy("act", SB_.ap, S_, [SST], [SB_])
        p3 = PB[7]
        for c in range(8):
            CG = C0 + c
            ks = slice(c * 128, (c + 1) * 128)
            col = slice(CG * NH + h, CG * NH + h + 1)
            kb.mm(p3[:, 0:128], KT[:, ks], SB_.ap, True, True, [KT, SB_], [p3])
            kb.stt("dve", ZB.ap, p3[:, 0:128], f2(nbg)[:, col], BV[:, c, :], ALU.mult, ALU.add, [p3, nbg, BV], [ZB])
            yield
            kb.mm(p3[:, 128:256], PM[:, c, :], ZB.ap, True, True, [PM, ZB], [p3])
            kb.copy("act", VN.ap, p3[:, 128:256], [p3], [VN])
            yield
            if half == 1:
                kb.mm(p3[:, 256:384], SB_.ap, QD[:, ks], True, False, [SB_, QD], [p3], sig=False)
                kb.mm(p3[:, 256:384], VN.ap, AT[:, c, :], False, True, [VN, AT], [p3])
                kb.copy("act", OT[:, ks], p3[:, 256:384], [p3], [OT])
            kb.mm(p3[:, 384:512], KD[:, c, :], VN.ap, True, True, [KD, VN], [p3])
            kb.stt("dve", SB_.ap, S_, f2(egl)[:, col], p3[:, 384:512], ALU.mult, ALU.add, [SST, egl, p3], [SB_])
            kb.stt("dve", S_, S_, f2(egl)[:, col], p3[:, 384:512], ALU.mult, ALU.add, [SST, egl, p3], [SST])
            yield
        if half == 1:
            kb.act(SQc.ap, OT.ap, AF.Square, [OT], [SQc])
            yield
            for tt in range(2):
                tsl = slice(tt * 512, (tt + 1) * 512)
                kb.mm(PB[7].ap, ones_b.ap, SQc[:, tsl], True, True, [ones_b, SQc], [PB[7]])
                kb.act(RNc[:, tsl], PB[7].ap, AF.Sqrt, [PB[7], eps_t], [RNc], bias=eps_t[:, 0:1], scale=1.0 / 128)
            yield
            kb.recip(RNc.ap, RNc.ap, [RNc], [RNc])
            kb.tt("dve", OT.ap, OT.ap, RNc.ap, ALU.mult, [OT, RNc], [OT])
            yield
            kb.stt("dve", OA[:, h, :], OT.ap, gng[:, 0:1], ZS.ap, ALU.mult, ALU.mult, [OT, gng, ZS], [OA])
            yield

    def run_pair(ga, gb, ra, rb):
        acc = 0.0
        da = db = False
        while not (da and db):
            if not db:
                try:
                    next(gb)
                except StopIteration:
                    db = True
            acc += ra / rb
            while acc >= 1.0 or (db and not da):
                acc -= 1.0
                if da:
                    acc = 0.0
                    break
                try:
                    next(ga)
                except StopIteration:
                    da = True
                    break

    def empty():
        return
        yield

    tasks = [(half, h) for half in range(2) for h in range(NH)]
    if ntasks is not None:
        tasks = tasks[:ntasks]
    if tasklist is not None:
        tasks = list(tasklist)
        ntasks = len(tasks)

    def chain(*gs):
        for g_ in gs:
            yield from g_

    def stream1(i):
        half, h = tasks[i]
        slot, WH = pf[i]
        SA_, CS_ = SETA[i % 2], CSET[i % 2]
        return chain(stageA(half, h, slot, WH, SA_, [("proj", 0), ("proj", 1), ("post", 0), ("proj", 2), ("post", 1), ("proj", 3), ("norm", 0), ("norm", 1)]),
                     merge([stageA(half, h, slot, WH, SA_, [("post", 2)]),
                            merge([stageB_batch(half, h, 0, SA_, CS_), stageB_batch(half, h, 1, SA_, CS_)])]))

    pf = {0: prefetch_head(*tasks[0])}
    for i in range(len(tasks) + 1):
        if i + 1 < len(tasks):
            pf[i + 1] = prefetch_head(*tasks[i + 1])
        if (ntasks is None and i == NH) or (ntasks is not None and i == 1):
            kb.free(XNP)
            gdn_state["OA"] = kb.sb("OA", [NH, TOK], BF16)
        ga = stream1(i) if i < len(tasks) else empty()
        gb = stageC(*tasks[i - 1], SETA[(i - 1) % 2], CSET[(i - 1) % 2]) if i >= 1 else empty()
        run_pair(gb, ga, 45.0, 100.0)
    OA = gdn_state["OA"]
    if stop_after_gdn:
        kb.drain()
        return kb
    kb.free(*RAWS, *CVS, *SQS, RN, SQ2, BV, KD, SB_, ZB, VN, OT)
    for s_ in SETA + CSET:
        kb.free(*s_.values())
    for s_ in BSET:
        kb.free(s_["GBC"], s_["ET"], s_["RG"], s_["ATN"], *s_["NM"])
    kb.free(SST, HALO, beta, nb, gtok, nbg, dkd, egl)
    inherit(PS, PB)
    WR.append(kb.sb("wr2", [8192], BF16))
    kb.dump("OA", OA, [NH, TOK], BF16)

    OB = kb.sb("OB", [KC, TOK], BF16)
    HS = kb.sb("HS", [2 + 1024], F32)
    CH = kb.sb("CH", [2 + 1024], F32)
    YC = kb.sb("YC", [1024], F32)
    for cc in range(KC):
        slot = wslot()
        W3 = slot.ap[:, 0:KC * 384].rearrange("p (k m n) -> p k m n", k=KC, m=3)
        for m, cbase in enumerate([C_SB, C_SC, C_SH]):
            wdma(W3[:, :, m, :], w_in_v[:, :, cbase + cc * 128:cbase + (cc + 1) * 128], slot)
        ph = psf(3)
        for m in (2, 1):
            for kc in range(KC):
                kb.mm(ph[:, (m - 1) * 2:(m - 1) * 2 + 2], W3[:, kc, m, :], XH[:, kc, :], kc == 0, kc == KC - 1,
                      [slot, XH], [PS[3]], sig=(kc == KC - 1))
        for m in (2, 1, 0):
            pacc = psf(m)
            for tt in range(2):
                for kc in range(KC):
                    kb.mm(pacc[:, tt * 512:(tt + 1) * 512], W3[:, kc, m, :], XN[:, kc, tt * 512:(tt + 1) * 512],
                          kc == 0, kc == KC - 1, [slot, XN], [PS[m]], sig=(kc == KC - 1))
        kb.copy("act", HS[:, 2:1026], psf(2), [PS[2]], [HS])
        kb.copy("act", HS[:, 0:2], ph[:, 2:4], [PS[3]], [HS])
        kb.tt("dve", CH[:, 2:1026], psf(1), HS[:, 2:1026], ALU.mult, [PS[1], HS], [CH])
        kb.tt("dve", CH[:, 0:2], ph[:, 0:2], HS[:, 0:2], ALU.mult, [PS[3], HS], [CH])
        wc = lambda i: cwsc[:, cc, i:i + 1]
        kb.ts("dve", YC.ap, CH[:, 0:1024], wc(0), None, ALU.mult, None, [CH, cwsc], [YC])
        kb.stt("dve", YC.ap, CH[:, 1:1025], wc(1), YC.ap, ALU.mult, ALU.add, [CH, cwsc, YC], [YC])
        kb.stt("dve", YC.ap, CH[:, 2:1026], wc(2), YC.ap, ALU.mult, ALU.add, [CH, cwsc, YC], [YC])
        kb.tt("dve", OB[:, cc, :], psf(0), YC.ap, ALU.mult, [PS[0], YC], [OB])
    kb.free(HS, CH, YC, XH)
    kb.dump("OB", OB, [KC, TOK], BF16)

    MG = kb.sb("MG", [KC, TOK], BF16)
    SA = [kb.sb(f"sa{i}", [512], F32) for i in range(2)]
    SBG = [kb.sb(f"sbg{i}", [512], F32) for i in range(2)]
    w_pa_v = w_pa.rearrange("(k p) n -> p k n", p=128)
    w_pb_v = w_pb.rearrange("(k p) n -> p k n", p=128)
    it = 0
    for cc in range(KC):
        slot = wslot()
        W4 = slot.ap.rearrange("p (k m n) -> p k m n", k=KC, m=4)
        cs_ = slice(cc * 128, (cc + 1) * 128)
        wdma(W4[:, :, 0, :], w_in_v[:, :, C_GA + cc * 128:C_GA + (cc + 1) * 128], slot)
        wdma(W4[:, :, 1, :], w_in_v[:, :, C_GB + cc * 128:C_GB + (cc + 1) * 128], slot)
        wdma(W4[:, :, 2, :], w_pa_v[:, :, cs_], slot)
        wdma(W4[:, :, 3, :], w_pb_v[:, :, cs_], slot)
        for tt in range(2):
            tsl = slice(tt * 512, (tt + 1) * 512)
            pi = (it % 2) * 2
            it += 1
            P01, P23 = PS[pi], PS[pi + 1]
            accs = [psf(pi)[:, 0:512], psf(pi)[:, 512:1024], psf(pi + 1)[:, 0:512], psf(pi + 1)[:, 512:1024]]
            srcs = [XN, XN, OA, OB]
            for m in range(4):
                PB_ = P01 if m < 2 else P23
                for kc in range(KC):
                    kb.mm(accs[m], W4[:, kc, m, :], srcs[m][:, kc, tsl], kc == 0, kc == KC - 1,
                          [slot, srcs[m]], [PB_], sig=(kc == KC - 1))
            sa, sbg = SA[it % 2], SBG[it % 2]
            kb.act(sa.ap, accs[0], AF.Sigmoid, [P01], [sa])
            kb.act(sbg.ap, accs[1], AF.Sigmoid, [P01], [sbg])
            kb.tt("dve", sa.ap, accs[2], sa.ap, ALU.mult, [P23, sa], [sa])
            kb.tt("dve", sbg.ap, accs[3], sbg.ap, ALU.mult, [P23, sbg], [sbg])
            kb.tt("dve", MG[:, cc, tsl], sa.ap, sbg.ap, ALU.add, [sa, sbg], [MG])
    kb.free(*SA, *SBG, OA, OB, XN)
    kb.dump("MG", MG, [KC, TOK], BF16)

    HT = kb.sb("HT", [KC, TOK], F32)
    XBL = [kb.sb(f"xbl{i}", [8, 128], F32) for i in range(2)]
    w_out_v = w_out.rearrange("(k p) n -> p k n", p=128)
    x_own_v = xg[1024:2048, :].rearrange("(t p) f -> p t f", p=128)
    for c4 in range(4):
        slot = wslot()
        WO = slot.ap.rearrange("p (k n) -> p k n", k=KC)
        wdma(WO, w_out_v[:, :, c4 * 512:(c4 + 1) * 512], slot)
        for cl in range(4):
            cc = c4 * 4 + cl
            xb = XBL[cc % 2]
            kb.dma("sp", xb.ap, x_own_v[:, :, cc * 128:(cc + 1) * 128], W=[xb])
            for tt in range(2):
                pi = (cc * 2 + tt) % 4
                acc = psf(pi)[:, 0:512]
                for kc in range(KC):
                    kb.mm(acc, WO[:, kc, cl * 128:(cl + 1) * 128], MG[:, kc, tt * 512:(tt + 1) * 512], kc == 0, False,
                          [slot, MG], [PS[pi]], sig=False)
                for j in range(4):
                    kb.mm(acc[:, j * 128:(j + 1) * 128], xb[:, tt * 4 + j, :], ident_f.ap, False, j == 3,
                          [xb, ident_f], [PS[pi]], sig=(j == 3))
                kb.copy("act" if tt == 0 else "dve", HT[:, cc, tt * 512:(tt + 1) * 512], acc, [PS[pi]], [HT])
    kb.free(MG, *XBL)
    kb.dump("HT1", HT, [KC, TOK], F32)

    def fm_rstd(RS):
        SQn = [kb.sb(f"sqn{i}", [1024], BF16) for i in range(2)]
        pn = psf(0)
        for kc in range(KC):
            q_ = SQn[kc % 2]
            kb.act(q_.ap, HT[:, kc, :], AF.Square, [HT], [q_])
            for tt in range(2):
                kb.mm(pn[:, tt * 512:(tt + 1) * 512], ones_b.ap, q_[:, tt * 512:(tt + 1) * 512], kc == 0, kc == KC - 1,
                      [ones_b, q_], [PS[0]], sig=(kc == KC - 1 or tt == 1))
        kb.act(RS.ap, pn, AF.Sqrt, [PS[0], eps_t], [RS], bias=eps_t[:, 0:1], scale=1.0 / D)
        kb.recip(RS.ap, RS.ap, [RS], [RS])
        kb.free(*SQn)

    RS = kb.sb("RS", [1024], F32)
    fm_rstd(RS)
    HN = kb.sb("HN", [KC, TOK], BF16)
    for kc in range(KC):
        kb.stt("dve", HN[:, kc, :], HT[:, kc, :], gffn[:, kc:kc + 1], RS.ap, ALU.mult, ALU.mult, [HT, gffn, RS], [HN])
    kb.dump("HN", HN, [KC, TOK], BF16)
    NG_ = 4
    GF = NFC // NG_
    FF = [kb.sb(f"ff{i}", [GF, TOK], BF16) for i in range(2)]
    SG = [kb.sb(f"sg{i}", [1024], F32) for i in range(2)]
    w_gate_v = w_gate.rearrange("(k p) n -> p k n", p=128)
    w_up_v = w_up.rearrange("(k p) n -> p k n", p=128)

    def down_group(g):
        ffb = FF[g % 2]
        for ccp in range(8):
            slot = wslot()
            WD = slot.ap[:, 0:GF * 256].rearrange("p (f n) -> p f n", f=GF)
            r0 = g * GF * 128
            wdma(WD, w_down[r0:r0 + GF * 128, ccp * 256:(ccp + 1) * 256].rearrange("(f p) n -> p f n", p=128), slot)
            for cl in range(2):
                cc = ccp * 2 + cl
                for tt in range(2):
                    pi = (cc * 2 + tt) % 4
                    acc = psf(pi)[:, 0:512]
                    for fcl in range(GF):
                        kb.mm(acc, WD[:, fcl, cl * 128:(cl + 1) * 128], ffb[:, fcl, tt * 512:(tt + 1) * 512],
                              fcl == 0, fcl == GF - 1, [slot, ffb], [PS[pi]], sig=(fcl == GF - 1))
                    hsl = HT[:, cc, tt * 512:(tt + 1) * 512]
                    kb.tt("dve", hsl, hsl, acc, ALU.add, [HT, PS[pi]], [HT])

    pending_groups = []
    for f2_ in range(NFC // 2):
        slot = wslot()
        WG = slot.ap.rearrange("p (k m n) -> p k m n", k=KC, m=2)
        c0 = f2_ * 256
        wdma(WG[:, :, 0, :], w_gate_v[:, :, c0:c0 + 256], slot)
        wdma(WG[:, :, 1, :], w_up_v[:, :, c0:c0 + 256], slot)
        for fl in range(2):
            fc = f2_ * 2 + fl
            g, fcl = fc // GF, fc % GF
            ffb = FF[g % 2]
            pg_i, pu_i = (0, 1) if fl == 0 else (2, 3)
            for m, pi in ((0, pg_i), (1, pu_i)):
                for tt in range(2):
                    for kc in range(KC):
                        kb.mm(psf(pi)[:, tt * 512:(tt + 1) * 512], WG[:, kc, m, fl * 128:(fl + 1) * 128],
                              HN[:, kc, tt * 512:(tt + 1) * 512], kc == 0, kc == KC - 1, [slot, HN], [PS[pi]],
                              sig=(kc == KC - 1))
            sg = SG[fl]
            kb.act(sg.ap, psf(pg_i), AF.Silu, [PS[pg_i]], [sg])
            kb.tt("dve", ffb[:, fcl, :], psf(pu_i), sg.ap, ALU.mult, [PS[pu_i], sg], [ffb])
            if fcl == GF - 1:
                pending_groups.append(g)
        for g in pending_groups:
            down_group(g)
        pending_groups = []
    kb.free(HN, *FF, *SG)
    kb.dump("HT2", HT, [KC, TOK], F32)

    fm_rstd(RS)
    for kc in range(KC):
        kb.stt("dve", HT[:, kc, :], HT[:, kc, :], gfin[:, kc:kc + 1], RS.ap, ALU.mult, ALU.mult, [HT, gfin, RS], [HT])
    YO = [kb.sb(f"yo{i}", [D], F32) for i in range(2)]
    k_ = 0
    for t in range(8):
        yo = YO[t % 2]
        for g4 in range(4):
            pi = k_ % 4
            k_ += 1
            pt = psf(pi)[:, 0:512].rearrange("p (a b) -> p a b", a=4)
            for j in range(4):
                kc = g4 * 4 + j
                kb.tr(pt[:, j, :], HT[:, kc, t * 128:(t + 1) * 128], ident_f.ap, [HT, ident_f], [PS[pi]], sig=(j == 3))
            kb.copy("act" if g4 % 2 == 0 else "dve", yo[:, g4 * 512:(g4 + 1) * 512],
                    psf(pi)[:, 0:512], [PS[pi]], [yo])
        kb.dma("sp", y[t * 128:(t + 1) * 128, :], yo.ap, R=[yo])
    kb.drain()
    return kb


_CACHE = {}


def _prep_inputs(inp):
    f = lambda a: np.ascontiguousarray(np.asarray(a, dtype=np.float32))
    x = f(inp["x"])
    L = 0
    pcol = lambda v: np.ascontiguousarray(v.reshape(-1, 128).T)
    shared = {
        "w_in": f(inp["w_in"][L]), "w_proj_a": f(inp["w_proj_a"][L]), "w_proj_b": f(inp["w_proj_b"][L]),
        "w_out": f(inp["w_out"][L]), "w_gate": f(inp["w_gate"][L]), "w_up": f(inp["w_up"][L]),
        "w_down": f(inp["w_down"][L]),
        "gmix_row": np.ascontiguousarray(np.broadcast_to(f(inp["ln_mix_g"][L])[None, :], (128, D))),
        "gffn_p": pcol(f(inp["ln_ffn_g"][L])),
        "gfin_p": pcol(f(inp["ln_final_g"])),
        "cw_qkv": np.ascontiguousarray(f(inp["conv_qkv_w"][L]).T.reshape(48, 128, 4).transpose(1, 0, 2)),
        "cw_sc": np.ascontiguousarray(f(inp["conv_sc_w"][L]).T.reshape(KC, 128, 3).transpose(1, 0, 2)),
        "alog_row": np.ascontiguousarray(np.broadcast_to(f(inp["A_log"][L])[None, :], (128, NH))),
        "dtb_row": np.ascontiguousarray(np.broadcast_to(f(inp["dt_bias"][L])[None, :], (128, NH))),
        "gng_p": np.ascontiguousarray(f(inp["gdn_norm_g"][L]).reshape(128, 1)),
    }
    in_maps = []
    zeros = np.zeros((1024, D), np.float32)
    for c in range(8):
        b, s = c // 2, c % 2
        if s == 0:
            xgc = np.concatenate([zeros, x[b, :1024]], axis=0)
        else:
            xgc = x[b]
        m = dict(shared)
        m["xg"] = np.ascontiguousarray(xgc)
        in_maps.append(m)
    return in_maps


def kernel(**inputs):
    if "kb" not in _CACHE:
        _CACHE["kb"] = build()
    kb = _CACHE["kb"]
    in_maps = _prep_inputs(inputs)
    res = run_bass_kernel_spmd(kb.nc, in_maps, core_ids=list(range(8)))
    out = np.empty((4, 2048, D), np.float32)
    for c in range(8):
        b, s = c // 2, c % 2
        out[b, s * 1024:(s + 1) * 1024] = res.results[c]["y"]
    return out
```
